# Optimizing a Trainium2 kernel written in Bass

```python
import math
import jax, jax.numpy as jnp
from jax import lax
import numpy as np

D_MODEL = 1024
BATCH = 8
SEQ = 2048
DEPTH = 2
DEC_BATCH = 128
DEC_SEQ = 8
PAST_LEN = 16384
PAGE_SIZE = 128

LRU_WIDTH = 3 * D_MODEL // 2
LRU_BLOCKS = 8
LRU_BLOCK = LRU_WIDTH // LRU_BLOCKS
LRU_CONV = 4
LRU_C = 8.0
GLA_HEADS = 4
GLA_DK = D_MODEL // (2 * GLA_HEADS)
GLA_DV = D_MODEL // GLA_HEADS
GLA_RANK = 16
GLA_TAU = 16.0
GLA_CHUNK = 64
D_FF = ((8 * D_MODEL // 3 + 127) // 128) * 128
FFN_CONV = 3
NORM_EPS = 1e-6
IN_SIZES = (LRU_WIDTH, LRU_WIDTH, GLA_HEADS * GLA_DK, GLA_HEADS * GLA_DK, GLA_HEADS * GLA_DV,
            GLA_HEADS * GLA_DV, GLA_RANK, D_MODEL, D_MODEL)
N_IN = sum(IN_SIZES)

kernel_name = "hybrid_rglru_gla_convffn_adaln_step"


def _rmsnorm(x, g):
    x32 = x.astype(jnp.float32)
    r = lax.rsqrt(jnp.mean(x32 * x32, axis=-1, keepdims=True) + NORM_EPS)
    return (x32 * r * g.astype(jnp.float32)).astype(x.dtype)


def _split_in(z):
    idx, acc = [], 0
    for s in IN_SIZES[:-1]:
        acc += s
        idx.append(acc)
    return jnp.split(z, idx, axis=-1)


def _causal_conv(x, buf, w, b):
    K = w.shape[0]
    T = x.shape[1]
    xp = jnp.concatenate([buf.astype(x.dtype), x], axis=1)
    y = b + sum(xp[:, j:j + T] * w[j] for j in range(K))
    return y, xp[:, xp.shape[1] - (K - 1):]


def _lin_combine(e1, e2):
    a1, b1 = e1
    a2, b2 = e2
    return a1 * a2, a2 * b1 + b2


def _rg_lru(x, h0, wa, ba, wx, bx, lam, reset_first):
    B, T, W = x.shape
    x32 = x.astype(jnp.float32)
    xb = x32.reshape(B, T, LRU_BLOCKS, LRU_BLOCK)
    r = jax.nn.sigmoid(jnp.einsum('btnc,ncd->btnd', xb, wa.astype(jnp.float32)).reshape(B, T, W) + ba)
    i = jax.nn.sigmoid(jnp.einsum('btnc,ncd->btnd', xb, wx.astype(jnp.float32)).reshape(B, T, W) + bx)
    log_a = -LRU_C * r * jax.nn.softplus(-lam.astype(jnp.float32))
    a = jnp.exp(log_a)
    mult = jnp.sqrt(-jnp.expm1(2.0 * log_a))
    if reset_first:
        mult = mult.at[:, 0].set(1.0)
    A, Bc = lax.associative_scan(_lin_combine, (a, mult * i * x32), axis=1)
    h = A * h0.astype(jnp.float32)[:, None] + Bc
    return h, h[:, -1]


def _gla(q, k, v, log_a, S0):
    B, T, H, _ = q.shape
    C = math.gcd(T, GLA_CHUNK)
    n = T // C

    def chunks(t):
        return t.astype(jnp.float32).reshape(B, n, C, H, -1).transpose(1, 0, 3, 2, 4)

    q, k, v, la = chunks(q), chunks(k), chunks(v), chunks(log_a)
    b = jnp.cumsum(la, axis=3)
    b_last = b[:, :, :, -1:]
    qd = q * jnp.exp(b) * (GLA_DK ** -0.5)
    kd = k * jnp.exp(-b)
    ke = k * jnp.exp(b_last - b)
    mask = jnp.tril(jnp.ones((C, C), dtype=bool))
    A = jnp.where(mask, jnp.einsum('nbhtk,nbhsk->nbhts', qd, kd), 0.0)
    o_intra = jnp.einsum('nbhts,nbhsv->nbhtv', A, v)

    def step(S, inp):
        qd_c, ke_c, v_c, bl_c = inp
        o = jnp.einsum('bhtk,bhkv->bhtv', qd_c, S)
        S = jnp.exp(bl_c[:, :, 0, :])[..., None] * S + jnp.einsum('bhsk,bhsv->bhkv', ke_c, v_c)
        return S, o

    S_fin, o_inter = lax.scan(step, S0.astype(jnp.float32), (qd, ke, v, b_last))
    o = (o_intra + o_inter).transpose(1, 0, 3, 2, 4).reshape(B, T, H, GLA_DV)
    return o, S_fin


def _mixer(h, conv_buf, h0, S0, p, reset_first):
    B, T, _ = h.shape
    z = h @ p['w_in']
    xl, gl, q, k, v, og, lr, mga, mgb = _split_in(z)
    xc, new_conv = _causal_conv(xl, conv_buf, p['lru_conv_w'], p['lru_conv_b'])
    hl, h_last = _rg_lru(xc, h0, p['lru_wa'], p['lru_ba'], p['lru_wx'], p['lru_bx'], p['lru_lambda'], reset_first)
    yA = hl.astype(h.dtype) * jax.nn.gelu(gl)
    logit = (lr @ p['gla_w_lr'] + p['gla_b_lr']).astype(jnp.float32)
    log_alpha = (jax.nn.log_sigmoid(logit) / GLA_TAU).reshape(B, T, GLA_HEADS, GLA_DK)
    o, S_fin = _gla(q.reshape(B, T, GLA_HEADS, GLA_DK), k.reshape(B, T, GLA_HEADS, GLA_DK),
                    v.reshape(B, T, GLA_HEADS, GLA_DV), log_alpha, S0)
    o = _rmsnorm(o, p['gla_norm_g']).astype(h.dtype)
    yB = (o * jax.nn.silu(og.reshape(B, T, GLA_HEADS, GLA_DV))).reshape(B, T, GLA_HEADS * GLA_DV)
    merged = jax.nn.sigmoid(mga) * (yA @ p['w_branch_a']) + jax.nn.sigmoid(mgb) * (yB @ p['w_branch_b'])
    return merged @ p['w_out'], new_conv, h_last, S_fin


def _ffn(h, ffn_buf, p):
    u = h @ p['ffn_w_up']
    ug, uv = jnp.split(u, [D_FF], axis=-1)
    ugc, new_buf = _causal_conv(ug, ffn_buf, p['ffn_conv_w'], p['ffn_conv_b'])
    return (jax.nn.silu(ugc) * uv) @ p['ffn_w_down'], new_buf


def _layer(x, c, h0, conv_buf, S0, ffn_buf, p, reset_first):
    mod = (jax.nn.silu(c) @ p['ada_w'] + p['ada_b'])[:, None, :]
    sh1, sc1, g1, sh2, sc2, g2 = jnp.split(mod, 6, axis=-1)
    h = _rmsnorm(x, p['norm_mix_g']) * (1 + sc1) + sh1
    m, new_conv, h_last, S_fin = _mixer(h, conv_buf, h0, S0, p, reset_first)
    x = x + g1 * m
    h = _rmsnorm(x, p['norm_ffn_g']) * (1 + sc2) + sh2
    f, new_ffn = _ffn(h, ffn_buf, p)
    x = x + g2 * f
    return x, h_last.astype(h0.dtype), new_conv.astype(conv_buf.dtype), S_fin.astype(S0.dtype), new_ffn.astype(ffn_buf.dtype)


def setup_inputs(seed: int = 0) -> dict:
    key = jax.random.key(seed)
    ks = iter(jax.random.split(key, 40))
    f32 = jnp.float32

    def nrm(shape, scale=1.0):
        return jax.random.normal(next(ks), shape, f32) * scale

    D = D_MODEL
    a0 = jax.random.uniform(next(ks), (DEPTH, LRU_WIDTH), f32, 0.9, 0.999)
    s = a0 ** (1.0 / LRU_C)
    lam = jnp.log(s) - jnp.log1p(-s)
    return {
        'x_prompt': nrm((BATCH, SEQ, D)),
        'x_sample': nrm((DEC_BATCH, DEC_SEQ, D)),
        'c_prompt': nrm((BATCH, D)),
        'c_sample': nrm((DEC_BATCH, D)),
        'state_lru_h': nrm((DEPTH, DEC_BATCH, LRU_WIDTH)),
        'state_lru_conv': nrm((DEPTH, DEC_BATCH, LRU_CONV - 1, LRU_WIDTH)),
        'state_gla': nrm((DEPTH, DEC_BATCH, GLA_HEADS, GLA_DK, GLA_DV)),
        'state_ffn_conv': nrm((DEPTH, DEC_BATCH, FFN_CONV - 1, D_FF)),
        'ada_w': nrm((DEPTH, D, 6 * D), 0.3 * D ** -0.5),
        'ada_b': nrm((DEPTH, 6 * D), 0.01),
        'norm_mix_g': 1.0 + nrm((DEPTH, D), 0.02),
        'norm_ffn_g': 1.0 + nrm((DEPTH, D), 0.02),
        'w_in': nrm((DEPTH, D, N_IN), D ** -0.5),
        'lru_conv_w': nrm((DEPTH, LRU_CONV, LRU_WIDTH), LRU_CONV ** -0.5),
        'lru_conv_b': nrm((DEPTH, LRU_WIDTH), 0.01),
        'lru_wa': nrm((DEPTH, LRU_BLOCKS, LRU_BLOCK, LRU_BLOCK), LRU_BLOCK ** -0.5),
        'lru_ba': nrm((DEPTH, LRU_WIDTH), 0.01),
        'lru_wx': nrm((DEPTH, LRU_BLOCKS, LRU_BLOCK, LRU_BLOCK), LRU_BLOCK ** -0.5),
        'lru_bx': nrm((DEPTH, LRU_WIDTH), 0.01),
        'lru_lambda': lam,
        'gla_w_lr': nrm((DEPTH, GLA_RANK, GLA_HEADS * GLA_DK), GLA_RANK ** -0.5),
        'gla_b_lr': nrm((DEPTH, GLA_HEADS * GLA_DK), 0.01),
        'gla_norm_g': 1.0 + nrm((DEPTH, GLA_DV), 0.02),
        'w_branch_a': nrm((DEPTH, LRU_WIDTH, D), LRU_WIDTH ** -0.5),
        'w_branch_b': nrm((DEPTH, GLA_HEADS * GLA_DV, D), (GLA_HEADS * GLA_DV) ** -0.5),
        'w_out': nrm((DEPTH, D, D), D ** -0.5),
        'ffn_w_up': nrm((DEPTH, D, 2 * D_FF), D ** -0.5),
        'ffn_conv_w': nrm((DEPTH, FFN_CONV, D_FF), FFN_CONV ** -0.5),
        'ffn_conv_b': nrm((DEPTH, D_FF), 0.01),
        'ffn_w_down': nrm((DEPTH, D_FF, D), D_FF ** -0.5),
        'final_norm_g': 1.0 + nrm((D,), 0.02),
    }


def reference(x_prompt, x_sample, c_prompt, c_sample, state_lru_h, state_lru_conv, state_gla, state_ffn_conv,
              ada_w, ada_b, norm_mix_g, norm_ffn_g, w_in, lru_conv_w, lru_conv_b, lru_wa, lru_ba, lru_wx, lru_bx,
              lru_lambda, gla_w_lr, gla_b_lr, gla_norm_g, w_branch_a, w_branch_b, w_out, ffn_w_up, ffn_conv_w,
              ffn_conv_b, ffn_w_down, final_norm_g):
    xp, xs = x_prompt, x_sample
    dt = x_prompt.dtype
    hp_l, hs_l, cp_l, cs_l, sp_l, ss_l, fp_l, fs_l = [], [], [], [], [], [], [], []
    for l in range(DEPTH):
        p = {'ada_w': ada_w[l], 'ada_b': ada_b[l], 'norm_mix_g': norm_mix_g[l], 'norm_ffn_g': norm_ffn_g[l],
             'w_in': w_in[l], 'lru_conv_w': lru_conv_w[l], 'lru_conv_b': lru_conv_b[l], 'lru_wa': lru_wa[l],
             'lru_ba': lru_ba[l], 'lru_wx': lru_wx[l], 'lru_bx': lru_bx[l], 'lru_lambda': lru_lambda[l],
             'gla_w_lr': gla_w_lr[l], 'gla_b_lr': gla_b_lr[l], 'gla_norm_g': gla_norm_g[l],
             'w_branch_a': w_branch_a[l], 'w_branch_b': w_branch_b[l], 'w_out': w_out[l],
             'ffn_w_up': ffn_w_up[l], 'ffn_conv_w': ffn_conv_w[l], 'ffn_conv_b': ffn_conv_b[l],
             'ffn_w_down': ffn_w_down[l]}
        Bp = xp.shape[0]
        xp, hp, cp, sp, fp = _layer(
            xp, c_prompt,
            jnp.zeros((Bp, LRU_WIDTH), dt), jnp.zeros((Bp, LRU_CONV - 1, LRU_WIDTH), dt),
            jnp.zeros((Bp, GLA_HEADS, GLA_DK, GLA_DV), dt), jnp.zeros((Bp, FFN_CONV - 1, D_FF), dt),
            p, True)
        xs, hs, cs, ss, fs = _layer(xs, c_sample, state_lru_h[l], state_lru_conv[l], state_gla[l],
                                    state_ffn_conv[l], p, False)
        hp_l.append(hp); hs_l.append(hs); cp_l.append(cp); cs_l.append(cs)
        sp_l.append(sp); ss_l.append(ss); fp_l.append(fp); fs_l.append(fs)
    y_prompt = _rmsnorm(xp, final_norm_g)
    y_sample = _rmsnorm(xs, final_norm_g)
    return (y_prompt, y_sample,
            jnp.stack(hp_l), jnp.stack(hs_l),
            jnp.stack(cp_l), jnp.stack(cs_l),
            jnp.stack(sp_l), jnp.stack(ss_l),
            jnp.stack(fp_l), jnp.stack(fs_l))
```

```python
import numpy as np
from contextlib import ExitStack
import concourse.bass as bass
import concourse.mybir as mybir
from concourse.bass_utils import run_bass_kernel_spmd

F32 = mybir.dt.float32
BF16 = mybir.dt.bfloat16
AF = mybir.ActivationFunctionType
ALU = mybir.AluOpType
MUL, ADD = ALU.mult, ALU.add

NCORES = 8
D = 1024
LW, LC = 1536, 12
DFF, FC = 2816, 22
NH, DK, DV = 4, 128, 256
NIN = 8208
EPS = 1e-6
C_XL, C_GL, C_Q, C_K, C_V, C_OG, C_LR, C_MGA, C_MGB = 0, 1536, 3072, 3584, 4096, 5120, 6144, 6160, 7184
V_ADAB, V_GMIX, V_GFFN, V_LCW, V_LCB, V_LBA, V_LBX, V_LLAM, V_BLR, V_FCW, V_FCB, V_GFIN, NV = \
    0, 48, 56, 64, 112, 124, 136, 148, 160, 164, 230, 252, 260

GROUPS = [(0, 6, False), (6, 6, False), (12, 4, True)]
NMAX = 768


def group_n(g):
    return GROUPS[g][1] * 128 + (128 if GROUPS[g][2] else 0)


def group_blocks(g):
    t0, nt, smp = GROUPS[g]
    blocks = []
    c = 0
    rem = nt * 128
    while rem > 0:
        w = min(512, rem)
        blocks.append((c, w, "p"))
        c += w
        rem -= w
    if smp:
        blocks.append((c, 128, "s"))
    return blocks


def lru_pairs():
    pairs = []
    for g3 in range(4):
        k0, k1, k2 = 3 * g3, 3 * g3 + 1, 3 * g3 + 2
        for (k, m) in [(k0, k0), (k1, k0), (k0, k1), (k1, k1), (k2, k1), (k1, k2), (k2, k2)]:
            pairs.append((k, m))
    return pairs


LRU_PAIRS = lru_pairs()


ENGS = ("pe", "act", "dve", "pool", "sp")


class Buf:
    __slots__ = ("name", "w", "wprev", "r", "rd", "sem", "semcnt")

    def __init__(self, name=""):
        self.name = name
        self.w = []
        self.wprev = []
        self.r = {}
        self.rd = []
        self.sem = None
        self.semcnt = 0


class Ins:
    __slots__ = ("eng", "fn", "deps", "mark", "val", "sem", "is_dma")

    def __init__(self, eng, fn, is_dma=False):
        self.eng = eng
        self.fn = fn
        self.deps = []
        self.mark = False
        self.val = None
        self.sem = None
        self.is_dma = is_dma


class Prog:
    def __init__(self, nc):
        self.nc = nc
        self.q = {e: [] for e in ENGS}
        self.out_dmas = []
        self.n_sems = 0
        self.pending = {e: [] for e in ENGS}
        self.last = {e: None for e in ENGS}
        self.marks = []
        self.last_bar = []

    def mark(self, name):
        self.marks.append((name, {e: len(self.q[e]) for e in ENGS}))

    @staticmethod
    def _flat(x):
        out = []
        for b in x:
            if isinstance(b, (list, tuple)):
                out.extend(Prog._flat(b))
            else:
                out.append(b)
        return out

    def _deps(self, ins, reads, writes, join=False):
        reads = self._flat(reads)
        writes = self._flat(writes)
        deps = []
        for b in reads:
            deps.extend(b.w)
        for b in writes:
            if not join:
                deps.extend(b.w)
            else:
                deps.extend(b.wprev)
            deps.extend(b.r.values())
            deps.extend(b.rd)
        if self.pending[ins.eng]:
            deps.extend(self.pending[ins.eng])
            self.pending[ins.eng] = []
        seen = set()
        for d in deps:
            if d is ins or id(d) in seen:
                continue
            if (not d.is_dma) and (not ins.is_dma) and d.eng == "pe" and ins.eng == "pe":
                continue
            seen.add(id(d))
            ins.deps.append(d)
            if not d.is_dma:
                d.mark = True
        for b in reads:
            if ins.is_dma:
                b.rd.append(ins)
            else:
                b.r[ins.eng] = ins
        for b in writes:
            if join:
                b.w = b.w + [ins]
            else:
                b.wprev = list(b.w) + list(b.r.values()) + list(b.rd)
                b.w = [ins]
                b.r = {}
                b.rd = []

    def op(self, eng, fn, reads=(), writes=()):
        ins = Ins(eng, fn)
        self._deps(ins, reads, writes)
        self.q[eng].append(ins)
        self.last[eng] = ins
        return ins

    def dma(self, eng, out, in_, sbuf, reads=(), writes=(), is_out=False, join=False, after_barrier=False, **kw):
        ins = Ins(eng, None, is_dma=True)
        self._deps(ins, reads, writes, join=join)
        if after_barrier:
            for d in self.last_bar:
                if d not in ins.deps:
                    ins.deps.append(d)
                    d.mark = True
        if sbuf.sem is None:
            sbuf.sem = self.n_sems
            self.n_sems += 1
        sbuf.semcnt += 16
        ins.sem = sbuf.sem
        ins.val = sbuf.semcnt
        ins.fn = (out, in_, kw)
        self.q[eng].append(ins)
        if is_out:
            self.out_dmas.append(ins)
        return ins

    def barrier(self):
        lasts = [self.last[e] for e in ("pe", "act", "dve", "pool") if self.last[e] is not None]
        self.last_bar = lasts
        for e in ("pe", "act", "dve"):
            for d in lasts:
                if d.eng != e:
                    self.pending[e].append(d)

    def emit(self, ctx):
        nc = self.nc
        eng_sem = {e: ctx.enter_context(nc.semaphore("s_" + e)) for e in ENGS}
        dsem = [ctx.enter_context(nc.semaphore("d%d" % i)) for i in range(self.n_sems)]
        fin = Ins("sp", lambda e: None)
        fin.deps.extend(self.out_dmas)
        self.q["sp"].append(fin)
        for e in ENGS:
            c = 0
            for ins in self.q[e]:
                if ins.is_dma:
                    continue
                if ins.mark:
                    c += 1
                    ins.val = c
        block = ctx.enter_context(nc.Block())
        stats = {}

        def run(engname, eng):
            seen = {}
            nw = 0
            for ins in self.q[engname]:
                need = {}
                for d in ins.deps:
                    if d.is_dma:
                        key = ("d", d.sem)
                    else:
                        key = ("e", d.eng)
                    v = d.val
                    assert v is not None
                    if v > need.get(key, 0):
                        need[key] = v
                for key, v in need.items():
                    if seen.get(key, 0) >= v:
                        continue
                    seen[key] = v
                    s = dsem[key[1]] if key[0] == "d" else eng_sem[key[1]]
                    eng.wait_ge(s, v)
                    nw += 1
                if ins.is_dma:
                    out, in_, kw = ins.fn
                    eng.dma_start(out=out, in_=in_, **kw).then_inc(dsem[ins.sem], 16)
                else:
                    r = ins.fn(eng)
                    if ins.mark:
                        assert r is not None
                        r.then_inc(eng_sem[engname], 1)
            stats[engname] = (len(self.q[engname]), nw)

        @block.tensor
        def _(e):
            run("pe", e)

        @block.scalar
        def _(e):
            run("act", e)

        @block.vector
        def _(e):
            run("dve", e)

        @block.gpsimd
        def _(e):
            run("pool", e)

        @block.sync
        def _(e):
            run("sp", e)

        return stats


class Rot:
    def __init__(self, items):
        self.items = items
        self.i = 0

    def get(self):
        x = self.items[self.i % len(self.items)]
        self.i += 1
        return x


def build_nc():
    nc = bass.Bass("TRN2", target_bir_lowering=False)
    P = Prog(nc)

    def din(name, shape):
        return nc.dram_tensor(name, list(shape), F32, kind="ExternalInput").ap()

    def dout(name, shape):
        return nc.dram_tensor(name, list(shape), F32, kind="ExternalOutput").ap()

    NG = len(GROUPS)
    dx = [din("x%d" % g, (128, 8, group_n(g))) for g in range(NG)]
    dcT = din("cT", (128, 8, 17))
    dvec = din("vec", (2, 128, NV))
    dgn = din("gn", (2, 128, 256))
    dada = din("ada_w", (2, 1024, 6144))
    dwin = din("w_in", (2, 1024, NIN))
    dwa = din("wa_x", (2, 128, 28, 128))
    dwx = din("wx_x", (2, 128, 28, 128))
    dwlr = din("w_lr", (2, 16, 512))
    dwba = din("w_ba_t", (2, 8, 128, 12, 128))
    dwbb = din("w_bb_t", (2, 8, 128, 8, 128))
    dwmg = din("w_mg_t", (2, 16, 128, 8, 128))
    dwxg = din("w_xg_t", (2, 24, 128, 8, 128))
    dwo = din("w_o", (2, 1024, 1024))
    dwup = din("w_up_t", (2, 44, 128, 8, 128))
    dwdn = din("w_dn_t", (2, 8, 128, 22, 128))
    dlh = din("lh_s", (2, 128, 12, 16))
    dlc = din("lc_s", (2, 128, 12, 16, 3))
    dfc = din("fc_s", (2, 128, 22, 16, 2))
    dsg = din("sg_s", (2, 16, 4, 128, 256))
    oy = [dout("y%d" % g, (128, 8, group_n(g))) for g in range(NG)]
    olhp = dout("o_lh_p", (2, 128, 12))
    olhs = dout("o_lh_s", (2, 128, 12, 16))
    olcp = dout("o_lc_p", (2, 128, 12, 3))
    olcs = dout("o_lc_s", (2, 128, 12, 16, 3))
    ogl = dout("o_gl", (2, 17, 4, 128, 256))
    ofcp = dout("o_fc_p", (2, 128, 22, 2))
    ofcs = dout("o_fc_s", (2, 128, 22, 16, 2))

    with ExitStack() as ctx:
        def sb(name, shape, dt=F32):
            return ctx.enter_context(nc.sbuf_tensor(name, list(shape), dt))

        def psum(name, shape, dt=F32):
            return ctx.enter_context(nc.psum_tensor(name, list(shape), dt))

        xT = sb("xT", (128, 8, NMAX))
        hT = sb("hT", (128, 8, NMAX), BF16)
        U = sb("U", (128, 28 * NMAX), BF16)
        yAT = U[:, 0:12 * NMAX].rearrange("p (c n) -> p c n", n=NMAX)
        yBT = U[:, 12 * NMAX:20 * NMAX].rearrange("p (c n) -> p c n", n=NMAX)
        mrgT = U[:, 20 * NMAX:28 * NMAX].rearrange("p (c n) -> p c n", n=NMAX)
        sogT = mrgT
        vTM = U[:, 0:6 * 1024].rearrange("p (t n) -> p t n", n=1024)
        actT = U[:, 0:22 * NMAX].rearrange("p (c n) -> p c n", n=NMAX)
        NSLAB = 4
        slabs = Rot([(sb("slab%d" % i, (128, 4096), BF16), Buf("slab%d" % i)) for i in range(NSLAB)])
        FP = Rot([(sb("f%d" % i, (128, 512)), Buf("f%d" % i)) for i in range(10)])
        NBP = 9
        BPbig = sb("BPbig", (128, NBP * 512), BF16)
        BP = Rot([(BPbig[:, i * 512:(i + 1) * 512], Buf("b%d" % i)) for i in range(NBP)])
        KM = Rot([(sb("km%d" % i, (128, 128), BF16), Buf("km%d" % i)) for i in range(4)])
        XPp = Rot([(sb("xp%d" % i, (128, 528)), Buf("xp%d" % i)) for i in range(3)])
        SC = sb("SC", (128, 6144), BF16)
        QD = [(SC[:, h * 512:(h + 1) * 512], Buf("qd%d" % h)) for h in range(4)]
        KD = [(SC[:, 2048 + h * 512:2048 + (h + 1) * 512], Buf("kd%d" % h)) for h in range(4)]
        KE = [(SC[:, 4096 + h * 512:4096 + (h + 1) * 512], Buf("ke%d" % h)) for h in range(4)]
        XCs = [[(SC[:, i * 1024:(i + 1) * 1024].bitcast(F32), [Buf("xc%d" % i)]) for i in range(3)],
               [(BPbig[:, (3 + 2 * i) * 512:(5 + 2 * i) * 512].bitcast(F32), [BP.items[3 + 2 * i][1], BP.items[4 + 2 * i][1]]) for i in range(3)]]
        XCBs = [[(SC[:, 3072 + i * 512:3072 + (i + 1) * 512], [Buf("xcb%d" % i)]) for i in range(3)],
                [(BP.items[i][0], [BP.items[i][1]]) for i in range(3)]]
        vec = [sb("vecs%d" % l, (128, NV)) for l in range(2)]
        vecB = [Buf("vec%d" % l) for l in range(2)]
        gn = [sb("gns%d" % l, (128, 256)) for l in range(2)]
        gnB = [Buf() for l in range(2)]
        der = [sb("der%d" % l, (128, 64)) for l in range(2)]
        derB = [Buf() for l in range(2)]
        cT = sb("cT_sb", (128, 8, 17)); cTB = Buf()
        scT = sb("scT", (128, 8, 17), BF16); scTB = Buf()
        modT = [sb("modT%d" % l, (128, 48, 17)) for l in range(2)]
        modB = [Buf() for l in range(2)]
        A1 = [sb("A1_%d" % l, (128, 8, 17)) for l in range(2)]
        A2 = [sb("A2_%d" % l, (128, 8, 17)) for l in range(2)]
        AB = [Buf() for l in range(2)]
        S32 = [sb("S32_%d" % l, (128, 4, 256)) for l in range(2)]
        S32B = [[Buf() for h in range(4)] for l in range(2)]
        Sbf = sb("Sbf", (128, 4, 256), BF16)
        SbfB = [Buf() for h in range(4)]
        HS = [sb("HS%d" % l, (128, 12)) for l in range(2)]
        HSB = [[Buf() for c in range(12)] for l in range(2)]
        LH = [sb("LH%d" % l, (128, 12, 3)) for l in range(2)]
        LHB = [[Buf() for c in range(12)] for l in range(2)]
        FH = [sb("FH%d" % l, (128, 22, 2)) for l in range(2)]
        FHB = [[Buf() for c in range(22)] for l in range(2)]
        identf = sb("identf", (128, 128)); identb = sb("identb", (128, 128), BF16)
        onesb = sb("onesb", (128, 128), BF16)
        maskP = sb("maskP", (128, 128)); maskS = sb("maskS", (128, 128))
        M16 = sb("M16", (128, 16))
        Rp = sb("Rp", (128, 512)); Rs = sb("Rs", (128, 128))
        constB = Buf("const")
        lrT = sb("lrT", (16, NMAX), BF16); lrTB = Buf()
        wlri = [sb("wlri%d" % l, (128, 8, 16), BF16) for l in range(2)]
        wlr = [sb("wlr%d" % l, (16, 512), BF16) for l in range(2)]
        wsmB = Buf("wsmall")
        smpT = sb("smp", (128, 1536)); smpB = Buf("smp")
        lhsB = lcsB = fcsB = lhoB = lcoB = fcoB = smpB
        lhs_t = smpT[:, 0:192].rearrange("p (c s) -> p c s", s=16)
        lho_t = smpT[:, 192:384].rearrange("p (c s) -> p c s", s=16)
        lcs_t = smpT[:, 384:960].rearrange("p (c s j) -> p c s j", s=16, j=3)
        lco_t = smpT[:, 960:1536].rearrange("p (c s j) -> p c s j", s=16, j=3)
        fcs_t = smpT[:, 0:704].rearrange("p (c s j) -> p c s j", s=16, j=2)
        fco_t = smpT[:, 704:1408].rearrange("p (c s j) -> p c s j", s=16, j=2)
        NSQ = 2
        S0f = Rot([(sb("S0f%d" % i, (128, NSQ, 256))[:], Buf(), []) for i in range(2)] +
                  [(smpT[:, i * 512:(i + 1) * 512].rearrange("p (s v) -> p s v", v=256), Buf(), [smpB]) for i in range(2)])
        S0b = Rot([(sb("S0b%d" % i, (128, NSQ, 256), BF16)[:], Buf(), []) for i in range(2)] +
                  [(smpT[:, 1024 + i * 256:1024 + (i + 1) * 256].bitcast(BF16).rearrange("p (s v) -> p s v", v=256), Buf(), [smpB]) for i in range(2)])
        qdm = U[:, 6144:8192].rearrange("p (j t) -> p j t", t=128); qdmB = Buf()
        Dt = sb("Dt", (128, 4, 16)); DtB = [Buf() for h in range(4)]
        ssq = sb("ssq", (128, 24)); ssqB = [Buf(), Buf()]
        junk = sb("junk", (128, 256)); junkB = Buf()
        PS = Rot([(psum("ps%d" % i, (128, 512)), [Buf("ps%d" % i)] * 4) for i in range(6)])
        Ops = psum("Ops", (128, 1024)); OB = [Buf(), Buf()]
        PS8 = Rot(PS.items + [(Ops[:, 0:512], OB[0]), (Ops[:, 512:1024], OB[1])])
        SCg = Rot([(SC[:, i * 2048:(i + 1) * 2048], Buf("scg%d" % i)) for i in range(3)])

        _bufs = {}

        def B(*key):
            b = _bufs.get(key)
            if b is None:
                b = Buf(str(key))
                _bufs[key] = b
            return b

        def act(out, in_, func, r, w, bias=None, scale=None, accum=None, eng="act"):
            kw = {}
            if bias is not None:
                kw["bias"] = bias
            if scale is not None:
                kw["scale"] = scale
            if accum is not None:
                kw["accum_out"] = accum
            return P.op(eng, lambda e: e.activation(out=out, in_=in_, func=func, **kw), r, w)

        def tt(out, in0, in1, op, r, w, eng="dve"):
            return P.op(eng, lambda e: e.tensor_tensor(out=out, in0=in0, in1=in1, op=op), r, w)

        def ts(out, in0, s1, s2, op0, op1, r, w, eng="dve"):
            if s2 is None:
                return P.op(eng, lambda e: e.tensor_scalar(out=out, in0=in0, scalar1=s1, scalar2=None, op0=op0), r, w)
            return P.op(eng, lambda e: e.tensor_scalar(out=out, in0=in0, scalar1=s1, scalar2=s2, op0=op0, op1=op1), r, w)

        def stt(out, in0, scalar, in1, op0, op1, r, w):
            return P.op("dve", lambda e: e.scalar_tensor_tensor(out=out, in0=in0, scalar=scalar, in1=in1, op0=op0, op1=op1), r, w)

        def mm(out, lhsT, rhs, start, stop, r, w):
            return P.op("pe", lambda e: e.matmul(out, lhsT=lhsT, rhs=rhs, start=start, stop=stop), r, w)

        def tr(out, in_, r, w):
            return P.op("pe", lambda e: e.transpose(out=out, in_=in_, identity=identb[:]), r, w)

        def cp(out, in_, r, w, eng="dve"):
            if eng == "act":
                return P.op("act", lambda e: e.copy(out=out, in_=in_), r, w)
            return P.op(eng, lambda e: e.tensor_copy(out=out, in_=in_), r, w)

        def ms(ap, val, w, eng="dve", r=()):
            return P.op(eng, lambda e: e.memset(ap, val), r, w)

        def scan(out, d0, d1, init, r, w):
            return P.op("dve", lambda e: e.tensor_tensor_scan(out=out, data0=d0, data1=d1, initial=init, op0=MUL, op1=ADD), r, w)

        def bc3(ap2, n):
            return ap2.unsqueeze(2).to_broadcast([ap2.shape[0], ap2.shape[1], n])

        def v3(ap2, t):
            return ap2.rearrange("p (j t) -> p j t", t=t)

        ms(identf[:], 1.0, [constB], eng="pool")
        P.op("pool", lambda e: e.affine_select(out=identf[:], in_=identf[:], pattern=[[1, 128]], compare_op=ALU.is_equal,
                                               fill=0.0, base=0, channel_multiplier=-1), [constB], [constB])
        cp(identb[:], identf[:], [constB], [constB], eng="pool")
        ms(onesb[:], 1.0, [constB], eng="pool")
        ms(maskP[:], 1.0, [constB], eng="pool")
        P.op("pool", lambda e: e.affine_select(out=maskP[:], in_=maskP[:], pattern=[[1, 128]], compare_op=ALU.is_ge,
                                               fill=0.0, base=0, channel_multiplier=-1), [constB], [constB])
        ms(maskS[:], 1.0, [constB], eng="pool")
        mS3 = v3(maskS[:], 8)
        P.op("pool", lambda e: e.affine_select(out=mS3, in_=mS3, pattern=[[-8, 16], [0, 8]], compare_op=ALU.is_ge,
                                               fill=0.0, base=0, channel_multiplier=1), [constB], [constB])
        P.op("pool", lambda e: e.affine_select(out=mS3, in_=mS3, pattern=[[8, 16], [0, 8]], compare_op=ALU.is_ge,
                                               fill=0.0, base=7, channel_multiplier=-1), [constB], [constB])
        P.op("pool", lambda e: e.affine_select(out=mS3, in_=mS3, pattern=[[8, 16], [1, 8]], compare_op=ALU.is_ge,
                                               fill=0.0, base=0, channel_multiplier=-1), [constB], [constB])
        ms(M16[:], 1.0, [constB], eng="pool")
        P.op("pool", lambda e: e.affine_select(out=M16[:], in_=M16[:], pattern=[[-8, 16]], compare_op=ALU.is_ge,
                                               fill=0.0, base=0, channel_multiplier=1), [constB], [constB])
        P.op("pool", lambda e: e.affine_select(out=M16[:], in_=M16[:], pattern=[[8, 16]], compare_op=ALU.is_ge,
                                               fill=0.0, base=7, channel_multiplier=-1), [constB], [constB])
        ms(Rp[:], 1.0, [constB], eng="pool")
        ms(v3(Rp[:], 128)[:, :, 0:1], 0.0, [constB], eng="pool", r=[constB])
        ms(Rs[:], 1.0, [constB], eng="pool")
        ms(v3(Rs[:], 8)[:, :, 0:1], 0.0, [constB], eng="pool", r=[constB])

        for l in range(2):
            P.dma("sp", vec[l][:], dvec[l], vecB[l], writes=[vecB[l]])
            P.dma("sp", gn[l][:], dgn[l], gnB[l], writes=[gnB[l]])
            P.dma("pool", wlr[l][:], dwlr[l], wsmB, writes=[wsmB])
            P.dma("pool", wlri[l][:], dwin[l][:, C_LR:C_LR + 16].rearrange("(k p) n -> p k n", p=128), wsmB, writes=[wsmB])
        P.dma("sp", cT[:], dcT, cTB, writes=[cTB])
        act(scT[:], cT[:], AF.Silu, [cTB], [scTB])
        for l in range(2):
            act(der[l][:, 0:12], vec[l][:, V_LLAM:V_LLAM + 12], AF.Exp, [vecB[l]], [derB[l]], scale=-1.0)
            act(der[l][:, 0:12], der[l][:, 0:12], AF.Ln, [derB[l]], [derB[l]], bias=1.0)
            ts(der[l][:, 12:24], der[l][:, 0:12], -8.0, None, MUL, None, [derB[l]], [derB[l]])
            ts(der[l][:, 0:12], der[l][:, 0:12], -4.0, None, MUL, None, [derB[l]], [derB[l]])
            ts(der[l][:, 24:28], vec[l][:, V_BLR:V_BLR + 4], -1.0, None, MUL, None, [vecB[l]], [derB[l]])
            ts(der[l][:, 28:40], vec[l][:, V_LBA:V_LBA + 12], 0.5, None, MUL, None, [vecB[l]], [derB[l]])
            ts(der[l][:, 40:52], vec[l][:, V_LBX:V_LBX + 12], 0.5, None, MUL, None, [vecB[l]], [derB[l]])

        def load_slab(pieces):
            t, b = slabs.get()
            for i, (dst_fn, src) in enumerate(pieces):
                P.dma("pool", dst_fn(t), src, b, writes=[b], join=(i > 0))
            return t, b

        def slab_k(t, kc, ncols):
            return t[:, 0:kc * ncols].rearrange("p (k n) -> p k n", n=ncols)

        def w_cols(dw, l, c0, ncols):
            return dw[l][:, c0:c0 + ncols].rearrange("(k p) n -> p k n", p=128)

        def tiled_src(dwt, l, m0, nm):
            return dwt[l, m0:m0 + nm].rearrange("m p k n -> p m (k n)")

        def tiled_dst(off, nm, kc):
            return lambda t: t[:, off:off + nm * kc * 128].rearrange("p (m x) -> p m x", x=kc * 128)

        def tiled_view(t, off, mi, kc):
            return t[:, off + mi * kc * 128:off + (mi + 1) * kc * 128].rearrange("p (k n) -> p k n", n=128)

        def load_k8(dw, l, c0, ncols):
            t, b = load_slab([(lambda t: slab_k(t, 8, ncols), w_cols(dw, l, c0, ncols))])
            return slab_k(t, 8, ncols), b

        def mod_steps(l):
            steps = [(lambda s=s: mod_slab(l, s)) for s in range(12)]
            steps.append(lambda: mod_fin(l))
            return steps

        def mod_stage(l):
            for f_ in mod_steps(l):
                f_()

        def mod_slab(l, s):
            if True:
                wv, wb = load_k8(dada, l, s * 512, 512)
                ps, pb = PS.get()
                pv = ps[:, 0:68].rearrange("p (m s) -> p m s", s=17)
                for mi in range(4):
                    for k in range(8):
                        mm(pv[:, mi, :], wv[:, k, mi * 128:(mi + 1) * 128], scT[:, k, :], k == 0, k == 7, [wb, scTB], [pb])
                tt(modT[l][:, 4 * s:4 * s + 4, :], pv, bc3(vec[l][:, V_ADAB + 4 * s:V_ADAB + 4 * s + 4], 17), ADD,
                   [pb, vecB[l]], [modB[l]])
        def mod_fin(l, which=(0, 1)):
            for (At, sc_off, gcol) in [((A1[l], 8, V_GMIX), (A2[l], 32, V_GFFN))[i] for i in which]:
                ts(At[:], modT[l][:, sc_off:sc_off + 8, :], 1.0, None, ADD, None, [modB[l]], [AB[l]])
                tt(At[:], At[:], bc3(vec[l][:, gcol:gcol + 8], 17), MUL, [AB[l], vecB[l]], [AB[l]])

        def rstd_block(g, c0, w, bidx):
            ss, ssB = PS.get()
            for k in range(8):
                sq, sqB = BP.get()
                act(sq[:, :w], xT[:, k, c0:c0 + w], AF.Square, [B("x", k, bidx)], [sqB])
                mm(ss[:, :w], onesb[:], sq[:, :w], k == 0, k == 7, [sqB, constB], [ssB])
            t1, t1B = FP.get()
            act(t1[:, :w], ss[:, :w], AF.Ln, [ssB, constB], [t1B], scale=1.0 / D, bias=epsc[:, 0:1])
            act(ss[:, :w], t1[:, :w], AF.Exp, [t1B, ssB], [ssB], scale=-0.5)
            return ss, ssB

        def norm_stage(l, g, At, sh_off):
            rss = [rstd_block(g, c0, w, bidx) for bidx, (c0, w, kind) in enumerate(group_blocks(g))]
            for bidx, (c0, w, kind) in enumerate(group_blocks(g)):
                rs, rsB = rss[bidx]
                for k in range(8):
                    tmp, tmpB = FP.get()
                    if kind == "p":
                        stt(tmp[:, :w], xT[:, k, c0:c0 + w], At[:, k, 0:1], rs[:, :w], MUL, MUL,
                            [B("x", k, bidx), rsB, AB[l]], [tmpB])
                        act(hT[:, k, c0:c0 + w], tmp[:, :w], AF.Identity, [tmpB, modB[l]], [B("h", k, bidx)],
                            bias=modT[l][:, sh_off + k, 0:1])
                    else:
                        tt(tmp[:, :w], xT[:, k, c0:c0 + w], rs[:, :w], MUL, [B("x", k, bidx), rsB], [tmpB])
                        t3 = v3(tmp[:, :w], 8)
                        tt(t3, t3, bc3(At[:, k, 1:17], 8), MUL, [tmpB, AB[l]], [tmpB])
                        tt(v3(hT[:, k, c0:c0 + w], 8), t3, bc3(modT[l][:, sh_off + k, 1:17], 8), ADD,
                           [tmpB, modB[l]], [B("h", k, bidx)])

        def hbufs(bidx):
            return [B("h", k, bidx) for k in range(8)]

        def fm_proj(ps, w, wv, wb, col0, c0, bidx, kc=8, src=None, srcbufs=None):
            src = hT if src is None else src
            for k in range(kc):
                rb = [wb] + ([B("h", k, bidx)] if srcbufs is None else srcbufs(k))
                mm(ps[0][:, :w], wv[:, k, col0:col0 + 128], src[:, k, c0:c0 + w], k == 0, k == kc - 1, rb, [ps[1]])

        def gla_stage(l, g, extra=()):
            t0, nt, smp = GROUPS[g]
            blocks = group_blocks(g)
            n = group_n(g)
            ntile = nt + (1 if smp else 0)
            for bidx, (c0, w, kind) in enumerate(blocks):
                ps = PS.get()
                for k in range(8):
                    mm(ps[0][0:16, :w], wlri[l][:, k, :], hT[:, k, c0:c0 + w], k == 0, k == 7, [wsmB, B("h", k, bidx)], [ps[1]])
                cp(lrT[0:16, c0:c0 + w], ps[0][0:16, :w], [ps[1]], [lrTB], eng="act")
            for s in range(2):
                wv, wb = load_k8(dwin, l, C_OG + s * 512, 512)
                for cc in range(4):
                    for bidx, (c0, w, kind) in enumerate(blocks):
                        ps = PS.get()
                        fm_proj(ps, w, wv, wb, cc * 128, c0, bidx)
                        act(sogT[:, s * 4 + cc, c0:c0 + w], ps[0][:, :w], AF.Silu, [ps[1]], [B("sog", bidx)])
            for s in range(2):
                wv, wb = load_k8(dwin, l, C_V + s * 512, 512)
                for t in range(ntile):
                    bidx = min(t // 4, len(blocks) - 1) if not (smp and t == nt) else len(blocks) - 1
                    ps = PS.get()
                    for k in range(8):
                        mm(ps[0][:], hT[:, k, t * 128:(t + 1) * 128], wv[:, k, :], k == 0, k == 7, [wb, B("h", k, bidx)], [ps[1]])
                    cp(vTM[:, t, s * 512:(s + 1) * 512], ps[0][:], [ps[1]], [B("v", t)], eng="act")
            for f_ in extra:
                f_()
            wq, wqB = load_k8(dwin, l, C_Q, 512)
            wk, wkB = load_k8(dwin, l, C_K, 512)
            if smp:
                ms(qdm[:], 0.0, [qdmB])
                ms(smpT[:, 0:1], 0.0, [smpB])
            for h in range(4):
                if t0 == 0:
                    ms(S32[l][:, h, :], 0.0, [S32B[l][h]])
                cp(Sbf[:, h, :], S32[l][:, h, :], [S32B[l][h]], [SbfB[h]], eng="act")
            for bidx, (c0, w, kind) in enumerate(blocks):
                ntb = w // 128
                tl = 128 if kind == "p" else 8
                nseg = w // tl
                R = Rp if kind == "p" else Rs
                hst = {}

                def H1(h):
                    L = PS.get()
                    mm(L[0][:, :w], wlr[l][0:16, h * 128:(h + 1) * 128], lrT[0:16, c0:c0 + w], True, True, [wsmB, lrTB], [L[1]])
                    e1, e1B = FP.get()
                    act(e1[:, :w], L[0][:, :w], AF.Exp, [L[1], derB[l]], [e1B], scale=-1.0, bias=der[l][:, 24 + h:25 + h])
                    act(e1[:, :w], e1[:, :w], AF.Ln, [e1B], [e1B], bias=1.0)
                    cum, cumB = FP.get()
                    scan(cum[:, :w], R[:, :w], e1[:, :w], 0.0, [e1B, constB], [cumB])
                    hst[h] = (cum, cumB)

                def H2(h):
                    cum, cumB = hst[h]
                    E1, E1B = FP.get()
                    E2, E2B = FP.get()
                    act(E1[:, :w], cum[:, :w], AF.Exp, [cumB], [E1B], scale=-1.0 / 16)
                    act(E2[:, :w], cum[:, :w], AF.Exp, [cumB], [E2B], scale=1.0 / 16)
                    E3, E3B = FP.get()
                    tt(v3(E3[:, :w], tl), v3(E2[:, :w], tl), v3(E1[:, :w], tl)[:, :, tl - 1:tl].to_broadcast([128, nseg, tl]), MUL,
                       [E1B, E2B], [E3B])
                    cp(Dt[:, h, 0:nseg], v3(E1[:, :w], tl)[:, :, tl - 1], [E1B], [DtB[h]])
                    Q = PS.get()
                    fm_proj(Q, w, wq, wqB, h * 128, c0, bidx)
                    stt(QD[h][0][:, :w], Q[0][:, :w], float(DK) ** -0.5, E1[:, :w], MUL, MUL, [Q[1], E1B], [QD[h][1]])
                    K = PS.get()
                    fm_proj(K, w, wk, wkB, h * 128, c0, bidx)
                    tt(KD[h][0][:, :w], K[0][:, :w], E2[:, :w], MUL, [K[1], E2B], [KD[h][1]])
                    tt(KE[h][0][:, :w], K[0][:, :w], E3[:, :w], MUL, [K[1], E3B], [KE[h][1]])

                H1(0)
                for h in range(4):
                    if h + 1 < 4:
                        H1(h + 1)
                    H2(h)
                qd, kd, ke = QD, KD, KE
                mask = maskP if kind == "p" else maskS
                tst = {}

                def TA(j):
                    cs = slice(j * 128, (j + 1) * 128)
                    ATb = PS.get()
                    KTb = PS.get()
                    kes_t, kes_B = BP.get()
                    atm_t, atm_B = BP.get()
                    for h in range(4):
                        KT = KTb[0][:, 128 * h:128 * h + 64].bitcast(BF16)
                        tr(KT, ke[h][0][:, cs], [ke[h][1], constB], [KTb[1][h]])
                        mm(ATb[0][:, 128 * h:128 * h + 128], kd[h][0][:, cs], qd[h][0][:, cs], True, True, [kd[h][1], qd[h][1]], [ATb[1][h]])
                    for h in range(4):
                        KT = KTb[0][:, 128 * h:128 * h + 64].bitcast(BF16)
                        cp(kes_t[:, 128 * h:128 * h + 128], KT, [KTb[1][h]], [kes_B], eng="act")
                        tt(atm_t[:, 128 * h:128 * h + 128], ATb[0][:, 128 * h:128 * h + 128], mask[:], MUL, [ATb[1][h], constB], [atm_B])
                    tst[j] = (kes_t, kes_B, atm_t, atm_B)

                def TB(j):
                    t = c0 // 128 + j
                    cs = slice(j * 128, (j + 1) * 128)
                    kes_t, kes_B, atm_t, atm_B = tst[j]
                    KVb = [PS.get(), PS.get()]
                    for h in range(4):
                        ob = OB[h // 2]
                        osl = Ops[:, h * 256:(h + 1) * 256]
                        vsl = vTM[:, t, h * 256:(h + 1) * 256]
                        mm(osl, atm_t[:, 128 * h:128 * h + 128], vsl, True, False, [atm_B, B("v", t)], [ob])
                        mm(osl, qd[h][0][:, cs], Sbf[:, h, :], False, True, [qd[h][1], SbfB[h]], [ob])
                        kvb = KVb[h % 2]
                        hq = 2 * (h // 2)
                        KV = kvb[0][:, 128 * hq:128 * hq + 256]
                        kvB = [kvb[1][hq], kvb[1][hq + 1]]
                        mm(KV, kes_t[:, 128 * h:128 * h + 128], vsl, True, True, [kes_B, B("v", t)], kvB)
                        stt(S32[l][:, h, :], S32[l][:, h, :], Dt[:, h, j:j + 1], KV, MUL, ADD, [kvB, DtB[h], S32B[l][h]], [S32B[l][h]])
                        cp(Sbf[:, h, :], S32[l][:, h, :], [S32B[l][h]], [SbfB[h]], eng="act")

                def TBs(j):
                    t = c0 // 128 + j
                    cs = slice(j * 128, (j + 1) * 128)
                    kes_t, kes_B, atm_t, atm_B = tst[j]
                    for h in range(4):
                        ob = OB[h // 2]
                        osl = Ops[:, h * 256:(h + 1) * 256]
                        vsl = vTM[:, t, h * 256:(h + 1) * 256]
                        qdiag = bass.AP(U, 6144, [[28 * NMAX, 128], [136, 16], [1, 8]])
                        cp(qdiag, v3(qd[h][0][:, cs], 8), [qd[h][1]], [qdmB])
                        mm(osl, atm_t[:, 128 * h:128 * h + 128], vsl, True, False, [atm_B, B("v", t)], [ob])
                        for q4 in range(16 // NSQ):
                            sf, sfB, sfX = S0f.get()
                            s0b, s0bB, s0bX = S0b.get()
                            src = dsg[l, NSQ * q4:NSQ * q4 + NSQ, h].rearrange("s k v -> k s v")
                            P.dma("sp", sf, src, sfB, reads=sfX, writes=[sfB])
                            cp(s0b, sf, [sfB] + sfX + s0bX, [s0bB], eng="act")
                            for jj in range(NSQ):
                                sq_ = NSQ * q4 + jj
                                mm(osl, qdm[:, sq_, :], s0b[:, jj, :], False, sq_ == 15, [qdmB, s0bB] + s0bX, [ob])
                            kms, kvs = [], []
                            for jj in range(NSQ):
                                sq_ = NSQ * q4 + jj
                                km_t, km_B = KM.get()
                                act(km_t[:], kes_t[:, 128 * h:128 * h + 128], AF.Copy, [kes_B, constB], [km_B], scale=M16[:, sq_:sq_ + 1])
                                kms.append((km_t, km_B))
                            for jj in range(NSQ):
                                kvb = PS.get()
                                mm(kvb[0][:, 0:256], kms[jj][0][:], vsl, True, True, [kms[jj][1], B("v", t)], [kvb[1]])
                                kvs.append(kvb)
                            for jj in range(NSQ):
                                sq_ = NSQ * q4 + jj
                                stt(sf[:, jj, :], sf[:, jj, :], Dt[:, h, sq_:sq_ + 1], kvs[jj][0][:, 0:256], MUL, ADD,
                                    [kvs[jj][1], DtB[h], sfB] + sfX, [sfB])
                            P.dma("pool", ogl[l, 1 + NSQ * q4:1 + NSQ * q4 + NSQ, h].rearrange("s k v -> k s v"), sf, sfB,
                                  reads=[sfB] + sfX, is_out=True)

                def TN1(j):
                    po = 12 * (j % 2)
                    sB = ssqB[j % 2]
                    oa, oaB = FP.get()
                    ob_, obB = FP.get()
                    cp(oa[:], Ops[:, 0:512], [OB[0]], [oaB], eng="act")
                    cp(ob_[:], Ops[:, 512:1024], [OB[1]], [obB], eng="act")
                    for h in range(4):
                        src_t, src_B = (oa, oaB) if h < 2 else (ob_, obB)
                        act(junk[:], src_t[:, (h % 2) * 256:(h % 2) * 256 + 256], AF.Square, [src_B, junkB], [junkB, sB],
                            accum=ssq[:, po + h:po + h + 1])
                    act(ssq[:, po + 4:po + 8], ssq[:, po:po + 4], AF.Ln, [sB, constB], [sB], scale=1.0 / DV, bias=epsc[:, 0:1])
                    act(ssq[:, po + 8:po + 12], ssq[:, po + 4:po + 8], AF.Exp, [sB], [sB], scale=-0.5)
                    tst[("o", j)] = (oa, oaB, ob_, obB)

                def TN2a(j):
                    po = 12 * (j % 2)
                    sB = ssqB[j % 2]
                    oa, oaB, ob_, obB = tst[("o", j)]
                    on_t, on_B = FP.get()
                    onb = on_t[:].bitcast(BF16)
                    for h in range(4):
                        src_t, src_B = (oa, oaB) if h < 2 else (ob_, obB)
                        stt(onb[:, h * 256:(h + 1) * 256], src_t[:, (h % 2) * 256:(h % 2) * 256 + 256], ssq[:, po + 8 + h:po + 9 + h], gn[l][:], MUL, MUL,
                            [src_B, sB, gnB[l]], [on_B])
                    tst[("on", j)] = (onb, on_B)

                def TN2b(j):
                    onb, on_B = tst[("on", j)]
                    yps = PS.get()
                    ypv = yps[0][:].bitcast(BF16).rearrange("p (c t) -> p c t", t=128)
                    for c in range(8):
                        tr(ypv[:, c, :], onb[:, c * 128:(c + 1) * 128], [on_B, constB], [yps[1]])
                    tt(yBT[:, 0:8, c0 + j * 128:c0 + (j + 1) * 128], ypv, sogT[:, 0:8, c0 + j * 128:c0 + (j + 1) * 128], MUL,
                       [yps[1], B("sog", bidx)], [B("yB", bidx)])

                TA(0)
                for j in range(ntb):
                    if kind == "p":
                        TB(j)
                    else:
                        TBs(j)
                    if j + 1 < ntb:
                        TA(j + 1)
                    TN1(j)
                    if j >= 1:
                        TN2a(j - 1)
                    if j >= 2:
                        TN2b(j - 2)
                TN2a(ntb - 1)
                for jj in range(max(0, ntb - 2), ntb):
                    TN2b(jj)
            if t0 + nt == 16:
                for h in range(4):
                    P.dma("sp", ogl[l, 0, h], S32[l][:, h, :], S32B[l][h], reads=[S32B[l][h]], is_out=True)

        def lru_stage(l, g):
            t0, nt, smp = GROUPS[g]
            blocks = group_blocks(g)
            FPT = FP.items
            if smp:
                P.dma("sp", lhs_t, dlh[l], lhsB, writes=[lhsB])
                P.dma("sp", lcs_t, dlc[l], lcsB, writes=[lcsB])
            if t0 == 0:
                for c in range(12):
                    ms(LH[l][:, c, :], 0.0, [LHB[l][c]])
                    ms(HS[l][:, c:c + 1], 0.0, [HSB[l][c]])
            cw = vec[l][:, V_LCW:V_LCW + 48]
            wts = {}

            def get_w(g3):
                if g3 not in wts:
                    wxl_t, wxlB = load_slab([(tiled_dst(0, 3, 8), tiled_src(dwxg, l, 3 * g3, 3)),
                                             (lambda t: t[:, 3072:3968].rearrange("p (k n) -> p k n", n=128), dwa[l][:, 7 * g3:7 * g3 + 7, :])])
                    wg_t, wgB = load_slab([(tiled_dst(0, 3, 8), tiled_src(dwxg, l, 12 + 3 * g3, 3)),
                                           (lambda t: t[:, 3072:3968].rearrange("p (k n) -> p k n", n=128), dwx[l][:, 7 * g3:7 * g3 + 7, :])])
                    wts[g3] = (wxl_t, wxlB, wg_t, wgB)
                return wts[g3]

            tbs = [(g3, bidx) for g3 in range(4) for bidx in range(len(blocks))]

            def p1a(ti):
                g3, bidx = tbs[ti]
                c0, w, kind = blocks[bidx]
                wxl_t, wxlB, _, _ = get_w(g3)
                st = ti % 2
                for ci in range(3):
                    c = 3 * g3 + ci
                    ps = PS8.get()
                    fm_proj(ps, w, tiled_view(wxl_t, 0, ci, 8), wxlB, 0, c0, bidx)
                    xp, xpB = XPp.get()
                    xc, xcB = XCs[st][ci]
                    cb = vec[l][:, V_LCB + c:V_LCB + c + 1]
                    if kind == "p":
                        cp(xp[:, 0:3], LH[l][:, c, :], [LHB[l][c]], [xpB], eng="act")
                        cp(xp[:, 3:3 + w], ps[0][:, :w], [ps[1]], [xpB], eng="act")
                        cp(LH[l][:, c, :], xp[:, w:w + 3], [xpB], [LHB[l][c]], eng="act")
                        ts(xc[:, :w], xp[:, 3:3 + w], cw[:, 3 * 12 + c:3 * 12 + c + 1], cb, MUL, ADD, [xpB, vecB[l]], xcB)
                        for jt in range(3):
                            stt(xc[:, :w], xp[:, jt:jt + w], cw[:, jt * 12 + c:jt * 12 + c + 1], xc[:, :w], MUL, ADD,
                                [xpB, vecB[l]] + xcB, xcB)
                    else:
                        xp3 = xp[:, 0:176].rearrange("p (j t) -> p j t", t=11)
                        cp(xp3[:, :, 0:3], lcs_t[:, c, :, :], [lcsB], [xpB], eng="act")
                        cp(xp3[:, :, 3:11], v3(ps[0][:, :w], 8), [ps[1]], [xpB], eng="act")
                        cp(lco_t[:, c, :, :], xp3[:, :, 8:11], [xpB], [lcoB], eng="act")
                        xc3 = v3(xc[:, :w], 8)
                        ts(xc3, xp3[:, :, 3:11], cw[:, 3 * 12 + c:3 * 12 + c + 1], cb, MUL, ADD, [xpB, vecB[l]], xcB)
                        for jt in range(3):
                            stt(xc3, xp3[:, :, jt:jt + 8], cw[:, jt * 12 + c:jt * 12 + c + 1], xc3, MUL, ADD,
                                [xpB, vecB[l]] + xcB, xcB)

            def p1b(ti):
                g3, bidx = tbs[ti]
                c0, w, kind = blocks[bidx]
                st = ti % 2
                for ci in range(3):
                    cp(XCBs[st][ci][0][:, :w], XCs[st][ci][0][:, :w], XCs[st][ci][1], XCBs[st][ci][1], eng="act")

            def p2(ti, mid):
                g3, bidx = tbs[ti]
                c0, w, kind = blocks[bidx]
                wxl_t, wxlB, wg_t, wgB = get_w(g3)
                wa = wxl_t[:, 3072:3968].rearrange("p (k n) -> p k n", n=128)
                wx = wg_t[:, 3072:3968].rearrange("p (k n) -> p k n", n=128)
                st = ti % 2
                XC, XCB = XCs[st], XCBs[st]
                Rl, Il, Gl = [], [], []
                for ci in range(3):
                    c = 3 * g3 + ci
                    prs = [(pi - 7 * g3, k) for pi, (k, m) in enumerate(LRU_PAIRS) if m == c]
                    Rps = PS8.get()
                    for ii, (pi, k) in enumerate(prs):
                        xb_t, xb_B = XCB[k - 3 * g3]
                        mm(Rps[0][:, :w], wa[:, pi, :], xb_t[:, :w], ii == 0, ii == len(prs) - 1, [wxlB] + xb_B, [Rps[1]])
                    Ips = PS8.get()
                    for ii, (pi, k) in enumerate(prs):
                        xb_t, xb_B = XCB[k - 3 * g3]
                        mm(Ips[0][:, :w], wx[:, pi, :], xb_t[:, :w], ii == 0, ii == len(prs) - 1, [wgB] + xb_B, [Ips[1]])
                    Rl.append(Rps); Il.append(Ips)
                for ci in range(3):
                    c = 3 * g3 + ci
                    T, TB = FPT[3 * ci]
                    act(T[:, :w], Rl[ci][0][:, :w], AF.Tanh, [Rl[ci][1], derB[l]], [TB], scale=0.5, bias=der[l][:, 28 + c:29 + c])
                for ci in range(3):
                    c = 3 * g3 + ci
                    I_, IB = FPT[3 * ci + 2]
                    act(I_[:, :w], Il[ci][0][:, :w], AF.Tanh, [Il[ci][1], derB[l]], [IB], scale=0.5, bias=der[l][:, 40 + c:41 + c])
                for ci in range(3):
                    c = 3 * g3 + ci
                    T, TB = FPT[3 * ci]
                    M_, MB = FPT[3 * ci + 1]
                    act(M_[:, :w], T[:, :w], AF.Exp, [TB, derB[l]], [MB], scale=der[l][:, 12 + c:13 + c], bias=der[l][:, 12 + c:13 + c])
                    act(T[:, :w], T[:, :w], AF.Exp, [TB, derB[l]], [TB], scale=der[l][:, c:c + 1], bias=der[l][:, c:c + 1])
                for ci in range(3):
                    M_, MB = FPT[3 * ci + 1]
                    act(M_[:, :w], M_[:, :w], AF.Sqrt, [MB], [MB], scale=-1.0, bias=1.0)
                mid()
                for ci in range(3):
                    Gps = PS8.get()
                    fm_proj(Gps, w, tiled_view(wg_t, 0, ci, 8), wgB, 0, c0, bidx)
                    Gl.append(Gps)
                for ci in range(3):
                    M_, MB = FPT[3 * ci + 1]
                    I_, IB = FPT[3 * ci + 2]
                    xc, xcB = XC[ci]
                    if kind == "p" and t0 == 0 and c0 == 0:
                        ms(M_[:, 0:1], 1.0, [MB], r=[MB])
                    stt(M_[:, :w], I_[:, :w], 1.0, M_[:, :w], ADD, MUL, [IB, MB], [MB])
                    stt(M_[:, :w], M_[:, :w], 0.5, xc[:, :w], MUL, MUL, [MB] + xcB, [MB])
                for ci in range(3):
                    I_, IB = FPT[3 * ci + 2]
                    act(I_[:, :w], Gl[ci][0][:, :w], AF.Gelu_apprx_tanh, [Gl[ci][1], IB], [IB])
                for ci in range(3):
                    c = 3 * g3 + ci
                    T, TB = FPT[3 * ci]
                    M_, MB = FPT[3 * ci + 1]
                    I_, IB = FPT[3 * ci + 2]
                    h_t, hB_ = XC[ci]
                    if kind == "p":
                        scan(h_t[:, :w], T[:, :w], M_[:, :w], HS[l][:, c:c + 1], [TB, MB, HSB[l][c]], hB_)
                        cp(HS[l][:, c:c + 1], h_t[:, w - 1:w], hB_, [HSB[l][c]])
                    else:
                        a3 = v3(T[:, :w], 8)
                        b3 = v3(M_[:, :w], 8)
                        tmp, tmpB = FPT[9]
                        tt(tmp[:, 0:16], a3[:, :, 0], lhs_t[:, c, :], MUL, [TB, lhsB], [tmpB])
                        tt(b3[:, :, 0], b3[:, :, 0], tmp[:, 0:16], ADD, [MB, tmpB], [MB])
                        ms(a3[:, :, 0:1], 0.0, [TB], r=[TB, tmpB])
                        scan(h_t[:, :w], T[:, :w], M_[:, :w], 0.0, [TB, MB], hB_)
                        cp(lho_t[:, c, :], v3(h_t[:, :w], 8)[:, :, 7], hB_, [lhoB])
                    tt(yAT[:, c, c0:c0 + w], h_t[:, :w], I_[:, :w], MUL, hB_ + [IB], [B("yA", c, bidx)])

            p1a(0)
            p1b(0)
            for ti in range(len(tbs)):
                nxt = ti + 1 < len(tbs)
                if nxt:
                    p1a(ti + 1)
                p2(ti, (lambda ti=ti: p1b(ti + 1)) if nxt else (lambda: None))
            if smp:
                P.dma("sp", olhs[l], lho_t, lhoB, reads=[lhoB], is_out=True)
                P.dma("sp", olcs[l], lco_t, lcoB, reads=[lcoB], is_out=True)
            if t0 + nt == 16:
                P.dma("sp", olhp[l], HS[l][:], HSB[l][0], reads=HSB[l], is_out=True)
                P.dma("sp", olcp[l], LH[l][:], LHB[l][0], reads=LHB[l], is_out=True)

        def merge_stage(l, g):
            blocks = group_blocks(g)
            for m in range(8):
                sx_t, sxB = load_slab([(tiled_dst(0, 1, 12), tiled_src(dwba, l, m, 1)),
                                       (tiled_dst(1536, 1, 8), tiled_src(dwbb, l, m, 1))])
                sa = tiled_view(sx_t, 0, 0, 12)
                sbv = tiled_view(sx_t, 1536, 0, 8)
                sg_t, sgB = SCg.get()
                P.dma("pool", tiled_dst(0, 1, 8)(sg_t), tiled_src(dwmg, l, m, 1), sgB, writes=[sgB], after_barrier=True)
                P.dma("pool", tiled_dst(1024, 1, 8)(sg_t), tiled_src(dwmg, l, 8 + m, 1), sgB, writes=[sgB], join=True, after_barrier=True)
                sga = tiled_view(sg_t, 0, 0, 8)
                sgb = tiled_view(sg_t, 1024, 0, 8)
                for bidx, (c0, w, kind) in enumerate(blocks):
                    PA = PS8.get()
                    fm_proj(PA, w, sa, sxB, 0, c0, bidx, kc=12, src=yAT, srcbufs=lambda k: [B("yA", k, bidx)])
                    PB = PS8.get()
                    fm_proj(PB, w, sbv, sxB, 0, c0, bidx, kc=8, src=yBT, srcbufs=lambda k: [B("yB", bidx)])
                    GA = PS8.get()
                    fm_proj(GA, w, sga, sgB, 0, c0, bidx)
                    GB = PS8.get()
                    fm_proj(GB, w, sgb, sgB, 0, c0, bidx)
                    ga, gaB = FP.get()
                    act(ga[:, :w], GA[0][:, :w], AF.Sigmoid, [GA[1]], [gaB])
                    gb, gbB = FP.get()
                    act(gb[:, :w], GB[0][:, :w], AF.Sigmoid, [GB[1]], [gbB])
                    tt(ga[:, :w], ga[:, :w], PA[0][:, :w], MUL, [gaB, PA[1]], [gaB])
                    tt(gb[:, :w], gb[:, :w], PB[0][:, :w], MUL, [gbB, PB[1]], [gbB])
                    tt(mrgT[:, m, c0:c0 + w], ga[:, :w], gb[:, :w], ADD, [gaB, gbB], [B("mrg", m, bidx)])

        def resid_update(l, ps, mo, c0, w, kind, bidx, gate_off):
            if kind == "p":
                stt(xT[:, mo, c0:c0 + w], ps[0][:, :w], modT[l][:, gate_off + mo, 0:1], xT[:, mo, c0:c0 + w], MUL, ADD,
                    [ps[1], modB[l], B("x", mo, bidx)], [B("x", mo, bidx)])
            else:
                tmp, tmpB = FP.get()
                tt(v3(tmp[:, :w], 8), v3(ps[0][:, :w], 8), bc3(modT[l][:, gate_off + mo, 1:17], 8), MUL, [ps[1], modB[l]], [tmpB])
                tt(xT[:, mo, c0:c0 + w], xT[:, mo, c0:c0 + w], tmp[:, :w], ADD, [tmpB, B("x", mo, bidx)], [B("x", mo, bidx)])

        def out_stage(l, g):
            blocks = group_blocks(g)
            for s in range(2):
                wv, wb = load_k8(dwo, l, s * 512, 512)
                for mi in range(4):
                    mo = 4 * s + mi
                    for bidx, (c0, w, kind) in enumerate(blocks):
                        ps = PS8.get()
                        fm_proj(ps, w, wv, wb, mi * 128, c0, bidx, src=mrgT, srcbufs=lambda k: [B("mrg", k, bidx)])
                        resid_update(l, ps, mo, c0, w, kind, bidx, 16)

        def ffn_stage(l, g, extra=()):
            extra = list(extra)
            t0, nt, smp = GROUPS[g]
            blocks = group_blocks(g)
            if smp:
                P.dma("sp", fcs_t, dfc[l], fcsB, writes=[fcsB])
            if t0 == 0:
                for f in range(22):
                    ms(FH[l][:, f, :], 0.0, [FHB[l][f]])
            cw = vec[l][:, V_FCW:V_FCW + 66]
            pend = []

            def stage_b(it):
                cv, cvB, UV, f, c0, w, bidx = it
                act(cv[:, :w], cv[:, :w], AF.Silu, [cvB], [cvB])
                tt(actT[:, f, c0:c0 + w], cv[:, :w], UV[0][:, :w], MUL, [cvB, UV[1]], [B("act", f, bidx)])

            for j in range(11):
                su_t, suB = load_slab([(tiled_dst(0, 2, 8), tiled_src(dwup, l, 2 * j, 2)),
                                       (tiled_dst(2048, 2, 8), tiled_src(dwup, l, 22 + 2 * j, 2))])
                for ci in range(2):
                    f = 2 * j + ci
                    cb = vec[l][:, V_FCB + f:V_FCB + f + 1]
                    for bidx, (c0, w, kind) in enumerate(blocks):
                        UG = PS8.get()
                        fm_proj(UG, w, tiled_view(su_t, 0, ci, 8), suB, 0, c0, bidx)
                        UV = PS8.get()
                        fm_proj(UV, w, tiled_view(su_t, 2048, ci, 8), suB, 0, c0, bidx)
                        xp, xpB = XPp.get()
                        cv, cvB = FP.get()
                        if kind == "p":
                            cp(xp[:, 0:2], FH[l][:, f, :], [FHB[l][f]], [xpB], eng="act")
                            cp(xp[:, 2:2 + w], UG[0][:, :w], [UG[1]], [xpB], eng="act")
                            cp(FH[l][:, f, :], xp[:, w:w + 2], [xpB], [FHB[l][f]], eng="act")
                            ts(cv[:, :w], xp[:, 2:2 + w], cw[:, 2 * 22 + f:2 * 22 + f + 1], cb, MUL, ADD, [xpB, vecB[l]], [cvB])
                            for jt in range(2):
                                stt(cv[:, :w], xp[:, jt:jt + w], cw[:, jt * 22 + f:jt * 22 + f + 1], cv[:, :w], MUL, ADD,
                                    [xpB, vecB[l], cvB], [cvB])
                        else:
                            xp3 = xp[:, 0:160].rearrange("p (j t) -> p j t", t=10)
                            cp(xp3[:, :, 0:2], fcs_t[:, f, :, :], [fcsB], [xpB], eng="act")
                            cp(xp3[:, :, 2:10], v3(UG[0][:, :w], 8), [UG[1]], [xpB], eng="act")
                            cp(fco_t[:, f, :, :], xp3[:, :, 8:10], [xpB], [fcoB], eng="act")
                            cv3 = v3(cv[:, :w], 8)
                            ts(cv3, xp3[:, :, 2:10], cw[:, 2 * 22 + f:2 * 22 + f + 1], cb, MUL, ADD, [xpB, vecB[l]], [cvB])
                            for jt in range(2):
                                stt(cv3, xp3[:, :, jt:jt + 8], cw[:, jt * 22 + f:jt * 22 + f + 1], cv3, MUL, ADD,
                                    [xpB, vecB[l], cvB], [cvB])
                        pend.append((cv, cvB, UV, f, c0, w, bidx))
                        if len(pend) > 1:
                            stage_b(pend.pop(0))
                if extra:
                    extra.pop(0)()
            while pend:
                stage_b(pend.pop(0))
            if smp:
                P.dma("sp", ofcs[l], fco_t, fcoB, reads=[fcoB], is_out=True)
            if t0 + nt == 16:
                P.dma("sp", ofcp[l], FH[l][:], FHB[l][0], reads=FHB[l], is_out=True)
            for mo in range(8):
                wd_t, wdB = load_slab([(lambda t: t[:, 0:1408], dwdn[l, mo].rearrange("p k n -> p (k n)")[:, 0:1408]),
                                       (lambda t: t[:, 1408:2816], dwdn[l, mo].rearrange("p k n -> p (k n)")[:, 1408:2816])])
                wd = slab_k(wd_t, 22, 128)
                for bidx, (c0, w, kind) in enumerate(blocks):
                    ps = PS8.get()
                    fm_proj(ps, w, wd, wdB, 0, c0, bidx, kc=22, src=actT, srcbufs=lambda k: [B("act", k, bidx)])
                    resid_update(l, ps, mo, c0, w, kind, bidx, 40)
                if extra:
                    extra.pop(0)()
            while extra:
                extra.pop(0)()

        def final_stage(g):
            rss = [rstd_block(g, c0, w, bidx) for bidx, (c0, w, kind) in enumerate(group_blocks(g))]
            for bidx, (c0, w, kind) in enumerate(group_blocks(g)):
                rs, rsB = rss[bidx]
                for k in range(8):
                    yo, yoB = FP.get()
                    stt(yo[:, :w], xT[:, k, c0:c0 + w], vec[1][:, V_GFIN + k:V_GFIN + k + 1], rs[:, :w], MUL, MUL,
                        [B("x", k, bidx), rsB, vecB[1]], [yoB])
                    P.dma("sp", oy[g][:, k, c0:c0 + w], yo[:, :w], yoB, reads=[yoB], is_out=True)

        epsc = sb("epsc", (128, 1))
        ms(epsc[:], EPS, [constB], eng="pool")
        for g in range(NG):
            n = group_n(g)
            xall = Buf("xload")
            for k in range(8):
                bl = [B("x", k, bidx) for bidx in range(len(group_blocks(g)))]
                P.dma("sp", xT[:, k, 0:n], dx[g][:, k, :], B("xdma", k), writes=bl)
            for l in range(2):
                late0 = ()
                if g == 0 and l == 0:
                    P.mark("mod g%d l%d" % (g, l))
                    for s_ in range(4):
                        mod_slab(0, s_)
                    mod_fin(0, which=(0,))
                    late0 = [(lambda s_=s_: mod_slab(0, s_)) for s_ in range(4, 12)] + [lambda: mod_fin(0, which=(1,))]
                P.mark("norm1 g%d l%d" % (g, l))
                norm_stage(l, g, A1[l], 0)
                P.barrier()
                P.mark("gla g%d l%d" % (g, l))
                gla_stage(l, g, extra=late0)
                P.barrier()
                P.mark("lru g%d l%d" % (g, l))
                lru_stage(l, g)
                P.barrier()
                P.mark("merge g%d l%d" % (g, l))
                merge_stage(l, g)
                P.mark("out g%d l%d" % (g, l))
                out_stage(l, g)
                P.mark("norm2 g%d l%d" % (g, l))
                norm_stage(l, g, A2[l], 24)
                P.barrier()
                P.mark("ffn g%d l%d" % (g, l))
                ffn_stage(l, g, extra=mod_steps(1) if (g == 0 and l == 0) else ())
            P.mark("final g%d" % g)
            final_stage(g)
        P.mark("end")
        stats = P.emit(ctx)
    return nc, (stats, P.marks)


_CACHE = {}


def _fm(a, nchunks):
    a = np.asarray(a)
    lead = a.shape[:-1]
    a = a.reshape(lead + (nchunks, 128))
    nd = a.ndim
    perm = (nd - 1, nd - 2) + tuple(range(nd - 2))
    return np.ascontiguousarray(a.transpose(perm))


def kernel(x_prompt, x_sample, c_prompt, c_sample, state_lru_h, state_lru_conv, state_gla, state_ffn_conv,
           ada_w, ada_b, norm_mix_g, norm_ffn_g, w_in, lru_conv_w, lru_conv_b, lru_wa, lru_ba, lru_wx, lru_bx,
           lru_lambda, gla_w_lr, gla_b_lr, gla_norm_g, w_branch_a, w_branch_b, w_out, ffn_w_up, ffn_conv_w,
           ffn_conv_b, ffn_w_down, final_norm_g):
    f32 = np.float32
    if "nc" not in _CACHE:
        _CACHE["nc"] = build_nc()
    nc, stats = _CACHE["nc"]

    vec = np.zeros((2, 128, NV), f32)
    for l in range(2):
        vec[l, :, V_ADAB:V_ADAB + 48] = np.asarray(ada_b[l], f32).reshape(48, 128).T
        vec[l, :, V_GMIX:V_GMIX + 8] = np.asarray(norm_mix_g[l], f32).reshape(8, 128).T
        vec[l, :, V_GFFN:V_GFFN + 8] = np.asarray(norm_ffn_g[l], f32).reshape(8, 128).T
        vec[l, :, V_LCW:V_LCW + 48] = np.asarray(lru_conv_w[l], f32).reshape(4, 12, 128).transpose(2, 0, 1).reshape(128, 48)
        vec[l, :, V_LCB:V_LCB + 12] = np.asarray(lru_conv_b[l], f32).reshape(12, 128).T
        vec[l, :, V_LBA:V_LBA + 12] = np.asarray(lru_ba[l], f32).reshape(12, 128).T
        vec[l, :, V_LBX:V_LBX + 12] = np.asarray(lru_bx[l], f32).reshape(12, 128).T
        vec[l, :, V_LLAM:V_LLAM + 12] = np.asarray(lru_lambda[l], f32).reshape(12, 128).T
        vec[l, :, V_BLR:V_BLR + 4] = np.asarray(gla_b_lr[l], f32).reshape(4, 128).T
        vec[l, :, V_FCW:V_FCW + 66] = np.asarray(ffn_conv_w[l], f32).reshape(3, 22, 128).transpose(2, 0, 1).reshape(128, 66)
        vec[l, :, V_FCB:V_FCB + 22] = np.asarray(ffn_conv_b[l], f32).reshape(22, 128).T
        vec[l, :, V_GFIN:V_GFIN + 8] = np.asarray(final_norm_g, f32).reshape(8, 128).T
    gnb = np.ascontiguousarray(np.broadcast_to(np.asarray(gla_norm_g, f32)[:, None, :], (2, 128, 256)))

    def expand_gate(wg):
        wg = np.asarray(wg, f32)
        out = np.zeros((2, 128, 28, 128), f32)
        for l in range(2):
            full = np.zeros((LW, LW), f32)
            for nb in range(8):
                full[nb * 192:(nb + 1) * 192, nb * 192:(nb + 1) * 192] = wg[l, nb]
            for pi, (k, m) in enumerate(LRU_PAIRS):
                out[l, :, pi, :] = full[k * 128:(k + 1) * 128, m * 128:(m + 1) * 128]
        return out

    def retile(wm):
        wm = np.asarray(wm, f32)
        L, K, M = wm.shape
        return np.ascontiguousarray(wm.reshape(L, K // 128, 128, M // 128, 128).transpose(0, 3, 2, 1, 4))

    shared = {
        "vec": vec, "gn": gnb,
        "ada_w": np.ascontiguousarray(ada_w, f32), "w_in": np.ascontiguousarray(w_in, f32),
        "wa_x": expand_gate(lru_wa), "wx_x": expand_gate(lru_wx),
        "w_lr": np.ascontiguousarray(gla_w_lr, f32),
        "w_ba_t": retile(w_branch_a), "w_bb_t": retile(w_branch_b),
        "w_mg_t": retile(np.asarray(w_in, f32)[:, :, C_MGA:C_MGA + 2048]),
        "w_xg_t": retile(np.asarray(w_in, f32)[:, :, C_XL:C_XL + 3072]),
        "w_o": np.ascontiguousarray(w_out, f32), "w_up_t": retile(ffn_w_up),
        "w_dn_t": retile(ffn_w_down),
    }
    xp = np.asarray(x_prompt, f32)
    xs = np.asarray(x_sample, f32)
    in_maps = []
    for c in range(NCORES):
        m = dict(shared)
        sl = slice(16 * c, 16 * c + 16)
        xpT = _fm(xp[c], 8)
        xsT = _fm(xs[sl].reshape(128, D), 8)
        for g, (t0, nt, smp) in enumerate(GROUPS):
            parts = [xpT[:, :, t0 * 128:(t0 + nt) * 128]]
            if smp:
                parts.append(xsT)
            m["x%d" % g] = np.ascontiguousarray(np.concatenate(parts, axis=2))
        call = np.concatenate([np.asarray(c_prompt, f32)[c:c + 1], np.asarray(c_sample, f32)[sl]], axis=0)
        m["cT"] = _fm(call, 8)
        m["lh_s"] = np.stack([_fm(np.asarray(state_lru_h, f32)[l, sl], 12) for l in range(2)])
        m["lc_s"] = np.stack([_fm(np.asarray(state_lru_conv, f32)[l, sl], 12) for l in range(2)])
        m["fc_s"] = np.stack([_fm(np.asarray(state_ffn_conv, f32)[l, sl], 22) for l in range(2)])
        m["sg_s"] = np.ascontiguousarray(np.asarray(state_gla, f32)[:, sl])
        in_maps.append(m)

    res = run_bass_kernel_spmd(nc, in_maps, core_ids=list(range(NCORES)))
    R = res.results

    def unfm(a):
        nd = a.ndim
        perm = tuple(range(2, nd)) + (1, 0)
        b = a.transpose(perm)
        return b.reshape(b.shape[:-2] + (b.shape[-2] * 128,))

    y_prompt = np.zeros((8, 2048, D), f32)
    y_sample = np.zeros((128, 8, D), f32)
    lru_h_p = np.zeros((2, 8, LW), f32); lru_h_s = np.zeros((2, 128, LW), f32)
    lru_c_p = np.zeros((2, 8, 3, LW), f32); lru_c_s = np.zeros((2, 128, 3, LW), f32)
    gla_p = np.zeros((2, 8, NH, DK, DV), f32); gla_s = np.zeros((2, 128, NH, DK, DV), f32)
    ffn_p = np.zeros((2, 8, 2, DFF), f32); ffn_s = np.zeros((2, 128, 2, DFF), f32)
    for c in range(NCORES):
        r = R[c]
        sl = slice(16 * c, 16 * c + 16)
        for g, (t0, nt, smp) in enumerate(GROUPS):
            yg = unfm(r["y%d" % g])
            y_prompt[c, t0 * 128:(t0 + nt) * 128] = yg[0:nt * 128]
            if smp:
                y_sample[sl] = yg[nt * 128:nt * 128 + 128].reshape(16, 8, D)
        for l in range(2):
            lru_h_p[l, c] = unfm(r["o_lh_p"][l])
            lru_h_s[l, sl] = unfm(r["o_lh_s"][l])
            lru_c_p[l, c] = unfm(r["o_lc_p"][l])
            lru_c_s[l, sl] = unfm(r["o_lc_s"][l])
            ffn_p[l, c] = unfm(r["o_fc_p"][l])
            ffn_s[l, sl] = unfm(r["o_fc_s"][l])
            gla_p[l, c] = r["o_gl"][l, 0]; gla_s[l, sl] = r["o_gl"][l, 1:17]
    return (y_prompt, y_sample, lru_h_p, lru_h_s, lru_c_p, lru_c_s, gla_p, gla_s, ffn_p, ffn_s)
```

```python
import numpy as np
from contextlib import ExitStack
import concourse.bass as bass
import concourse.mybir as mybir
from concourse.bass_utils import run_bass_kernel_spmd

F32 = mybir.dt.float32
BF16 = mybir.dt.bfloat16
AF = mybir.ActivationFunctionType
ALU = mybir.AluOpType
MUL, ADD = ALU.mult, ALU.add

NCORES = 8
D = 1024
LW, LC = 1536, 12
DFF, FC = 2816, 22
NH, DK, DV = 4, 128, 256
NIN = 8208
EPS = 1e-6
C_XL, C_GL, C_Q, C_K, C_V, C_OG, C_LR, C_MGA, C_MGB = 0, 1536, 3072, 3584, 4096, 5120, 6144, 6160, 7184
V_ADAB, V_GMIX, V_GFFN, V_LCW, V_LCB, V_LBA, V_LBX, V_LLAM, V_BLR, V_FCW, V_FCB, V_GFIN, NV = \
    0, 48, 56, 64, 112, 124, 136, 148, 160, 164, 230, 252, 260

GROUPS = [(0, 6, False), (6, 6, False), (12, 4, True)]
NMAX = 768


def group_n(g):
    return GROUPS[g][1] * 128 + (128 if GROUPS[g][2] else 0)


def group_blocks(g):
    t0, nt, smp = GROUPS[g]
    blocks = []
    c = 0
    rem = nt * 128
    while rem > 0:
        w = min(512, rem)
        blocks.append((c, w, "p"))
        c += w
        rem -= w
    if smp:
        blocks.append((c, 128, "s"))
    return blocks


def lru_pairs():
    pairs = []
    for g3 in range(4):
        k0, k1, k2 = 3 * g3, 3 * g3 + 1, 3 * g3 + 2
        for (k, m) in [(k0, k0), (k1, k0), (k0, k1), (k1, k1), (k2, k1), (k1, k2), (k2, k2)]:
            pairs.append((k, m))
    return pairs


LRU_PAIRS = lru_pairs()


ENGS = ("pe", "act", "dve", "pool", "sp")


class Buf:
    __slots__ = ("name", "w", "wprev", "r", "rd", "sem", "semcnt")

    def __init__(self, name=""):
        self.name = name
        self.w = []
        self.wprev = []
        self.r = {}
        self.rd = []
        self.sem = None
        self.semcnt = 0


class Ins:
    __slots__ = ("eng", "fn", "deps", "mark", "val", "sem", "is_dma")

    def __init__(self, eng, fn, is_dma=False):
        self.eng = eng
        self.fn = fn
        self.deps = []
        self.mark = False
        self.val = None
        self.sem = None
        self.is_dma = is_dma


class Prog:
    def __init__(self, nc):
        self.nc = nc
        self.q = {e: [] for e in ENGS}
        self.out_dmas = []
        self.n_sems = 0
        self.pending = {e: [] for e in ENGS}
        self.last = {e: None for e in ENGS}
        self.marks = []
        self.last_bar = []

    def mark(self, name):
        self.marks.append((name, {e: len(self.q[e]) for e in ENGS}))

    @staticmethod
    def _flat(x):
        out = []
        for b in x:
            if isinstance(b, (list, tuple)):
                out.extend(Prog._flat(b))
            else:
                out.append(b)
        return out

    def _deps(self, ins, reads, writes, join=False):
        reads = self._flat(reads)
        writes = self._flat(writes)
        deps = []
        for b in reads:
            deps.extend(b.w)
        for b in writes:
            if not join:
                deps.extend(b.w)
            else:
                deps.extend(b.wprev)
            deps.extend(b.r.values())
            deps.extend(b.rd)
        if self.pending[ins.eng]:
            deps.extend(self.pending[ins.eng])
            self.pending[ins.eng] = []
        seen = set()
        for d in deps:
            if d is ins or id(d) in seen:
                continue
            if (not d.is_dma) and (not ins.is_dma) and d.eng == "pe" and ins.eng == "pe":
                continue
            seen.add(id(d))
            ins.deps.append(d)
            if not d.is_dma:
                d.mark = True
        for b in reads:
            if ins.is_dma:
                b.rd.append(ins)
            else:
                b.r[ins.eng] = ins
        for b in writes:
            if join:
                b.w = b.w + [ins]
            else:
                b.wprev = list(b.w) + list(b.r.values()) + list(b.rd)
                b.w = [ins]
                b.r = {}
                b.rd = []

    def op(self, eng, fn, reads=(), writes=()):
        ins = Ins(eng, fn)
        self._deps(ins, reads, writes)
        self.q[eng].append(ins)
        self.last[eng] = ins
        return ins

    def dma(self, eng, out, in_, sbuf, reads=(), writes=(), is_out=False, join=False, after_barrier=False, **kw):
        ins = Ins(eng, None, is_dma=True)
        self._deps(ins, reads, writes, join=join)
        if after_barrier:
            for d in self.last_bar:
                if d not in ins.deps:
                    ins.deps.append(d)
                    d.mark = True
        if sbuf.sem is None:
            sbuf.sem = self.n_sems
            self.n_sems += 1
        sbuf.semcnt += 16
        ins.sem = sbuf.sem
        ins.val = sbuf.semcnt
        ins.fn = (out, in_, kw)
        self.q[eng].append(ins)
        if is_out:
            self.out_dmas.append(ins)
        return ins

    def barrier(self):
        lasts = [self.last[e] for e in ("pe", "act", "dve", "pool") if self.last[e] is not None]
        self.last_bar = lasts
        for e in ("pe", "act", "dve"):
            for d in lasts:
                if d.eng != e:
                    self.pending[e].append(d)

    def emit(self, ctx):
        nc = self.nc
        eng_sem = {e: ctx.enter_context(nc.semaphore("s_" + e)) for e in ENGS}
        dsem = [ctx.enter_context(nc.semaphore("d%d" % i)) for i in range(self.n_sems)]
        fin = Ins("sp", lambda e: None)
        fin.deps.extend(self.out_dmas)
        self.q["sp"].append(fin)
        for e in ENGS:
            c = 0
            for ins in self.q[e]:
                if ins.is_dma:
                    continue
                if ins.mark:
                    c += 1
                    ins.val = c
        block = ctx.enter_context(nc.Block())
        stats = {}

        def run(engname, eng):
            seen = {}
            nw = 0
            for ins in self.q[engname]:
                need = {}
                for d in ins.deps:
                    if d.is_dma:
                        key = ("d", d.sem)
                    else:
                        key = ("e", d.eng)
                    v = d.val
                    assert v is not None
                    if v > need.get(key, 0):
                        need[key] = v
                for key, v in need.items():
                    if seen.get(key, 0) >= v:
                        continue
                    seen[key] = v
                    s = dsem[key[1]] if key[0] == "d" else eng_sem[key[1]]
                    eng.wait_ge(s, v)
                    nw += 1
                if ins.is_dma:
                    out, in_, kw = ins.fn
                    eng.dma_start(out=out, in_=in_, **kw).then_inc(dsem[ins.sem], 16)
                else:
                    r = ins.fn(eng)
                    if ins.mark:
                        assert r is not None
                        r.then_inc(eng_sem[engname], 1)
            stats[engname] = (len(self.q[engname]), nw)

        @block.tensor
        def _(e):
            run("pe", e)

        @block.scalar
        def _(e):
            run("act", e)

        @block.vector
        def _(e):
            run("dve", e)

        @block.gpsimd
        def _(e):
            run("pool", e)

        @block.sync
        def _(e):
            run("sp", e)

        return stats


class Rot:
    def __init__(self, items):
        self.items = items
        self.i = 0

    def get(self):
        x = self.items[self.i % len(self.items)]
        self.i += 1
        return x


def build_nc():
    nc = bass.Bass("TRN2", target_bir_lowering=False)
    P = Prog(nc)

    def din(name, shape):
        return nc.dram_tensor(name, list(shape), F32, kind="ExternalInput").ap()

    def dout(name, shape):
        return nc.dram_tensor(name, list(shape), F32, kind="ExternalOutput").ap()

    NG = len(GROUPS)
    dx = [din("x%d" % g, (128, 8, group_n(g))) for g in range(NG)]
    dcT = din("cT", (128, 8, 17))
    dvec = din("vec", (2, 128, NV))
    dgn = din("gn", (2, 128, 256))
    dada = din("ada_w", (2, 1024, 6144))
    dwin = din("w_in", (2, 1024, NIN))
    dwa = din("wa_x", (2, 128, 28, 128))
    dwx = din("wx_x", (2, 128, 28, 128))
    dwlr = din("w_lr", (2, 16, 512))
    dwba = din("w_ba_t", (2, 8, 128, 12, 128))
    dwbb = din("w_bb_t", (2, 8, 128, 8, 128))
    dwmg = din("w_mg_t", (2, 16, 128, 8, 128))
    dwxg = din("w_xg_t", (2, 24, 128, 8, 128))
    dwo = din("w_o", (2, 1024, 1024))
    dwup = din("w_up_t", (2, 44, 128, 8, 128))
    dwdn = din("w_dn_t", (2, 8, 128, 22, 128))
    dlh = din("lh_s", (2, 128, 12, 16))
    dlc = din("lc_s", (2, 128, 12, 16, 3))
    dfc = din("fc_s", (2, 128, 22, 16, 2))
    dsg = din("sg_s", (2, 16, 4, 128, 256))
    oy = [dout("y%d" % g, (128, 8, group_n(g))) for g in range(NG)]
    olhp = dout("o_lh_p", (2, 128, 12))
    olhs = dout("o_lh_s", (2, 128, 12, 16))
    olcp = dout("o_lc_p", (2, 128, 12, 3))
    olcs = dout("o_lc_s", (2, 128, 12, 16, 3))
    ogl = dout("o_gl", (2, 17, 4, 128, 256))
    ofcp = dout("o_fc_p", (2, 128, 22, 2))
    ofcs = dout("o_fc_s", (2, 128, 22, 16, 2))

    with ExitStack() as ctx:
        def sb(name, shape, dt=F32):
            return ctx.enter_context(nc.sbuf_tensor(name, list(shape), dt))

        def psum(name, shape, dt=F32):
            return ctx.enter_context(nc.psum_tensor(name, list(shape), dt))

        xT = sb("xT", (128, 8, NMAX))
        hT = sb("hT", (128, 8, NMAX), BF16)
        U = sb("U", (128, 28 * NMAX), BF16)
        yAT = U[:, 0:12 * NMAX].rearrange("p (c n) -> p c n", n=NMAX)
        yBT = U[:, 12 * NMAX:20 * NMAX].rearrange("p (c n) -> p c n", n=NMAX)
        mrgT = U[:, 20 * NMAX:28 * NMAX].rearrange("p (c n) -> p c n", n=NMAX)
        sogT = mrgT
        vTM = U[:, 0:6 * 1024].rearrange("p (t n) -> p t n", n=1024)
        actT = U[:, 0:22 * NMAX].rearrange("p (c n) -> p c n", n=NMAX)
        NSLAB = 4
        slabs = Rot([(sb("slab%d" % i, (128, 4096), BF16), Buf("slab%d" % i)) for i in range(NSLAB)])
        FP = Rot([(sb("f%d" % i, (128, 512)), Buf("f%d" % i)) for i in range(10)])
        NBP = 9
        BPbig = sb("BPbig", (128, NBP * 512), BF16)
        BP = Rot([(BPbig[:, i * 512:(i + 1) * 512], Buf("b%d" % i)) for i in range(NBP)])
        KM = Rot([(sb("km%d" % i, (128, 128), BF16), Buf("km%d" % i)) for i in range(4)])
        XPp = Rot([(sb("xp%d" % i, (128, 528)), Buf("xp%d" % i)) for i in range(3)])
        SC = sb("SC", (128, 6144), BF16)
        QD = [(SC[:, h * 512:(h + 1) * 512], Buf("qd%d" % h)) for h in range(4)]
        KD = [(SC[:, 2048 + h * 512:2048 + (h + 1) * 512], Buf("kd%d" % h)) for h in range(4)]
        KE = [(SC[:, 4096 + h * 512:4096 + (h + 1) * 512], Buf("ke%d" % h)) for h in range(4)]
        XCs = [[(SC[:, i * 1024:(i + 1) * 1024].bitcast(F32), [Buf("xc%d" % i)]) for i in range(3)],
               [(BPbig[:, (3 + 2 * i) * 512:(5 + 2 * i) * 512].bitcast(F32), [BP.items[3 + 2 * i][1], BP.items[4 + 2 * i][1]]) for i in range(3)]]
        XCBs = [[(SC[:, 3072 + i * 512:3072 + (i + 1) * 512], [Buf("xcb%d" % i)]) for i in range(3)],
                [(BP.items[i][0], [BP.items[i][1]]) for i in range(3)]]
        vec = [sb("vecs%d" % l, (128, NV)) for l in range(2)]
        vecB = [Buf("vec%d" % l) for l in range(2)]
        gn = [sb("gns%d" % l, (128, 256)) for l in range(2)]
        gnB = [Buf() for l in range(2)]
        der = [sb("der%d" % l, (128, 64)) for l in range(2)]
        derB = [Buf() for l in range(2)]
        cT = sb("cT_sb", (128, 8, 17)); cTB = Buf()
        scT = sb("scT", (128, 8, 17), BF16); scTB = Buf()
        modT = [sb("modT%d" % l, (128, 48, 17)) for l in range(2)]
        modB = [Buf() for l in range(2)]
        A1 = [sb("A1_%d" % l, (128, 8, 17)) for l in range(2)]
        A2 = [sb("A2_%d" % l, (128, 8, 17)) for l in range(2)]
        AB = [Buf() for l in range(2)]
        S32 = [sb("S32_%d" % l, (128, 4, 256)) for l in range(2)]
        S32B = [[Buf() for h in range(4)] for l in range(2)]
        Sbf = sb("Sbf", (128, 4, 256), BF16)
        SbfB = [Buf() for h in range(4)]
        HS = [sb("HS%d" % l, (128, 12)) for l in range(2)]
        HSB = [[Buf() for c in range(12)] for l in range(2)]
        LH = [sb("LH%d" % l, (128, 12, 3)) for l in range(2)]
        LHB = [[Buf() for c in range(12)] for l in range(2)]
        FH = [sb("FH%d" % l, (128, 22, 2)) for l in range(2)]
        FHB = [[Buf() for c in range(22)] for l in range(2)]
        identf = sb("identf", (128, 128)); identb = sb("identb", (128, 128), BF16)
        onesb = sb("onesb", (128, 128), BF16)
        maskP = sb("maskP", (128, 128)); maskS = sb("maskS", (128, 128))
        M16 = sb("M16", (128, 16))
        Rp = sb("Rp", (128, 512)); Rs = sb("Rs", (128, 128))
        constB = Buf("const")
        lrT = sb("lrT", (16, NMAX), BF16); lrTB = Buf()
        wlri = [sb("wlri%d" % l, (128, 8, 16), BF16) for l in range(2)]
        wlr = [sb("wlr%d" % l, (16, 512), BF16) for l in range(2)]
        wsmB = Buf("wsmall")
        smpT = sb("smp", (128, 1536)); smpB = Buf("smp")
        lhsB = lcsB = fcsB = lhoB = lcoB = fcoB = smpB
        lhs_t = smpT[:, 0:192].rearrange("p (c s) -> p c s", s=16)
        lho_t = smpT[:, 192:384].rearrange("p (c s) -> p c s", s=16)
        lcs_t = smpT[:, 384:960].rearrange("p (c s j) -> p c s j", s=16, j=3)
        lco_t = smpT[:, 960:1536].rearrange("p (c s j) -> p c s j", s=16, j=3)
        fcs_t = smpT[:, 0:704].rearrange("p (c s j) -> p c s j", s=16, j=2)
        fco_t = smpT[:, 704:1408].rearrange("p (c s j) -> p c s j", s=16, j=2)
        NSQ = 2
        S0f = Rot([(sb("S0f%d" % i, (128, NSQ, 256))[:], Buf(), []) for i in range(2)] +
                  [(smpT[:, i * 512:(i + 1) * 512].rearrange("p (s v) -> p s v", v=256), Buf(), [smpB]) for i in range(2)])
        S0b = Rot([(sb("S0b%d" % i, (128, NSQ, 256), BF16)[:], Buf(), []) for i in range(2)] +
                  [(smpT[:, 1024 + i * 256:1024 + (i + 1) * 256].bitcast(BF16).rearrange("p (s v) -> p s v", v=256), Buf(), [smpB]) for i in range(2)])
        qdm = U[:, 6144:8192].rearrange("p (j t) -> p j t", t=128); qdmB = Buf()
        Dt = sb("Dt", (128, 4, 16)); DtB = [Buf() for h in range(4)]
        ssq = sb("ssq", (128, 24)); ssqB = [Buf(), Buf()]
        junk = sb("junk", (128, 256)); junkB = Buf()
        PS = Rot([(psum("ps%d" % i, (128, 512)), [Buf("ps%d" % i)] * 4) for i in range(6)])
        Ops = psum("Ops", (128, 1024)); OB = [Buf(), Buf()]
        PS8 = Rot(PS.items + [(Ops[:, 0:512], OB[0]), (Ops[:, 512:1024], OB[1])])
        SCg = Rot([(SC[:, i * 2048:(i + 1) * 2048], Buf("scg%d" % i)) for i in range(3)])

        _bufs = {}

        def B(*key):
            b = _bufs.get(key)
            if b is None:
                b = Buf(str(key))
                _bufs[key] = b
            return b

        def act(out, in_, func, r, w, bias=None, scale=None, accum=None, eng="act"):
            kw = {}
            if bias is not None:
                kw["bias"] = bias
            if scale is not None:
                kw["scale"] = scale
            if accum is not None:
                kw["accum_out"] = accum
            return P.op(eng, lambda e: e.activation(out=out, in_=in_, func=func, **kw), r, w)

        def tt(out, in0, in1, op, r, w, eng="dve"):
            return P.op(eng, lambda e: e.tensor_tensor(out=out, in0=in0, in1=in1, op=op), r, w)

        def ts(out, in0, s1, s2, op0, op1, r, w, eng="dve"):
            if s2 is None:
                return P.op(eng, lambda e: e.tensor_scalar(out=out, in0=in0, scalar1=s1, scalar2=None, op0=op0), r, w)
            return P.op(eng, lambda e: e.tensor_scalar(out=out, in0=in0, scalar1=s1, scalar2=s2, op0=op0, op1=op1), r, w)

        def stt(out, in0, scalar, in1, op0, op1, r, w):
            return P.op("dve", lambda e: e.scalar_tensor_tensor(out=out, in0=in0, scalar=scalar, in1=in1, op0=op0, op1=op1), r, w)

        def mm(out, lhsT, rhs, start, stop, r, w):
            return P.op("pe", lambda e: e.matmul(out, lhsT=lhsT, rhs=rhs, start=start, stop=stop), r, w)

        def tr(out, in_, r, w):
            return P.op("pe", lambda e: e.transpose(out=out, in_=in_, identity=identb[:]), r, w)

        def cp(out, in_, r, w, eng="dve"):
            if eng == "act":
                return P.op("act", lambda e: e.copy(out=out, in_=in_), r, w)
            return P.op(eng, lambda e: e.tensor_copy(out=out, in_=in_), r, w)

        def ms(ap, val, w, eng="dve", r=()):
            return P.op(eng, lambda e: e.memset(ap, val), r, w)

        def scan(out, d0, d1, init, r, w):
            return P.op("dve", lambda e: e.tensor_tensor_scan(out=out, data0=d0, data1=d1, initial=init, op0=MUL, op1=ADD), r, w)

        def bc3(ap2, n):
            return ap2.unsqueeze(2).to_broadcast([ap2.shape[0], ap2.shape[1], n])

        def v3(ap2, t):
            return ap2.rearrange("p (j t) -> p j t", t=t)

        ms(identf[:], 1.0, [constB], eng="pool")
        P.op("pool", lambda e: e.affine_select(out=identf[:], in_=identf[:], pattern=[[1, 128]], compare_op=ALU.is_equal,
                                               fill=0.0, base=0, channel_multiplier=-1), [constB], [constB])
        cp(identb[:], identf[:], [constB], [constB], eng="pool")
        ms(onesb[:], 1.0, [constB], eng="pool")
        ms(maskP[:], 1.0, [constB], eng="pool")
        P.op("pool", lambda e: e.affine_select(out=maskP[:], in_=maskP[:], pattern=[[1, 128]], compare_op=ALU.is_ge,
                                               fill=0.0, base=0, channel_multiplier=-1), [constB], [constB])
        ms(maskS[:], 1.0, [constB], eng="pool")
        mS3 = v3(maskS[:], 8)
        P.op("pool", lambda e: e.affine_select(out=mS3, in_=mS3, pattern=[[-8, 16], [0, 8]], compare_op=ALU.is_ge,
                                               fill=0.0, base=0, channel_multiplier=1), [constB], [constB])
        P.op("pool", lambda e: e.affine_select(out=mS3, in_=mS3, pattern=[[8, 16], [0, 8]], compare_op=ALU.is_ge,
                                               fill=0.0, base=7, channel_multiplier=-1), [constB], [constB])
        P.op("pool", lambda e: e.affine_select(out=mS3, in_=mS3, pattern=[[8, 16], [1, 8]], compare_op=ALU.is_ge,
                                               fill=0.0, base=0, channel_multiplier=-1), [constB], [constB])
        ms(M16[:], 1.0, [constB], eng="pool")
        P.op("pool", lambda e: e.affine_select(out=M16[:], in_=M16[:], pattern=[[-8, 16]], compare_op=ALU.is_ge,
                                               fill=0.0, base=0, channel_multiplier=1), [constB], [constB])
        P.op("pool", lambda e: e.affine_select(out=M16[:], in_=M16[:], pattern=[[8, 16]], compare_op=ALU.is_ge,
                                               fill=0.0, base=7, channel_multiplier=-1), [constB], [constB])
        ms(Rp[:], 1.0, [constB], eng="pool")
        ms(v3(Rp[:], 128)[:, :, 0:1], 0.0, [constB], eng="pool", r=[constB])
        ms(Rs[:], 1.0, [constB], eng="pool")
        ms(v3(Rs[:], 8)[:, :, 0:1], 0.0, [constB], eng="pool", r=[constB])

        for l in range(2):
            P.dma("sp", vec[l][:], dvec[l], vecB[l], writes=[vecB[l]])
            P.dma("sp", gn[l][:], dgn[l], gnB[l], writes=[gnB[l]])
            P.dma("pool", wlr[l][:], dwlr[l], wsmB, writes=[wsmB])
            P.dma("pool", wlri[l][:], dwin[l][:, C_LR:C_LR + 16].rearrange("(k p) n -> p k n", p=128), wsmB, writes=[wsmB])
        P.dma("sp", cT[:], dcT, cTB, writes=[cTB])
        act(scT[:], cT[:], AF.Silu, [cTB], [scTB])
        for l in range(2):
            act(der[l][:, 0:12], vec[l][:, V_LLAM:V_LLAM + 12], AF.Exp, [vecB[l]], [derB[l]], scale=-1.0)
            act(der[l][:, 0:12], der[l][:, 0:12], AF.Ln, [derB[l]], [derB[l]], bias=1.0)
            ts(der[l][:, 12:24], der[l][:, 0:12], -8.0, None, MUL, None, [derB[l]], [derB[l]])
            ts(der[l][:, 0:12], der[l][:, 0:12], -4.0, None, MUL, None, [derB[l]], [derB[l]])
            ts(der[l][:, 24:28], vec[l][:, V_BLR:V_BLR + 4], -1.0, None, MUL, None, [vecB[l]], [derB[l]])
            ts(der[l][:, 28:40], vec[l][:, V_LBA:V_LBA + 12], 0.5, None, MUL, None, [vecB[l]], [derB[l]])
            ts(der[l][:, 40:52], vec[l][:, V_LBX:V_LBX + 12], 0.5, None, MUL, None, [vecB[l]], [derB[l]])

        def load_slab(pieces):
            t, b = slabs.get()
            for i, (dst_fn, src) in enumerate(pieces):
                P.dma("pool", dst_fn(t), src, b, writes=[b], join=(i > 0))
            return t, b

        def slab_k(t, kc, ncols):
            return t[:, 0:kc * ncols].rearrange("p (k n) -> p k n", n=ncols)

        def w_cols(dw, l, c0, ncols):
            return dw[l][:, c0:c0 + ncols].rearrange("(k p) n -> p k n", p=128)

        def tiled_src(dwt, l, m0, nm):
            return dwt[l, m0:m0 + nm].rearrange("m p k n -> p m (k n)")

        def tiled_dst(off, nm, kc):
            return lambda t: t[:, off:off + nm * kc * 128].rearrange("p (m x) -> p m x", x=kc * 128)

        def tiled_view(t, off, mi, kc):
            return t[:, off + mi * kc * 128:off + (mi + 1) * kc * 128].rearrange("p (k n) -> p k n", n=128)

        def load_k8(dw, l, c0, ncols):
            t, b = load_slab([(lambda t: slab_k(t, 8, ncols), w_cols(dw, l, c0, ncols))])
            return slab_k(t, 8, ncols), b

        def mod_steps(l):
            steps = [(lambda s=s: mod_slab(l, s)) for s in range(12)]
            steps.append(lambda: mod_fin(l))
            return steps

        def mod_stage(l):
            for f_ in mod_steps(l):
                f_()

        def mod_slab(l, s):
            if True:
                wv, wb = load_k8(dada, l, s * 512, 512)
                ps, pb = PS.get()
                pv = ps[:, 0:68].rearrange("p (m s) -> p m s", s=17)
                for mi in range(4):
                    for k in range(8):
                        mm(pv[:, mi, :], wv[:, k, mi * 128:(mi + 1) * 128], scT[:, k, :], k == 0, k == 7, [wb, scTB], [pb])
                tt(modT[l][:, 4 * s:4 * s + 4, :], pv, bc3(vec[l][:, V_ADAB + 4 * s:V_ADAB + 4 * s + 4], 17), ADD,
                   [pb, vecB[l]], [modB[l]])
        def mod_fin(l, which=(0, 1)):
            for (At, sc_off, gcol) in [((A1[l], 8, V_GMIX), (A2[l], 32, V_GFFN))[i] for i in which]:
                ts(At[:], modT[l][:, sc_off:sc_off + 8, :], 1.0, None, ADD, None, [modB[l]], [AB[l]])
                tt(At[:], At[:], bc3(vec[l][:, gcol:gcol + 8], 17), MUL, [AB[l], vecB[l]], [AB[l]])

        def rstd_block(g, c0, w, bidx):
            ss, ssB = PS.get()
            for k in range(8):
                sq, sqB = BP.get()
                act(sq[:, :w], xT[:, k, c0:c0 + w], AF.Square, [B("x", k, bidx)], [sqB])
                mm(ss[:, :w], onesb[:], sq[:, :w], k == 0, k == 7, [sqB, constB], [ssB])
            t1, t1B = FP.get()
            act(t1[:, :w], ss[:, :w], AF.Ln, [ssB, constB], [t1B], scale=1.0 / D, bias=epsc[:, 0:1])
            act(ss[:, :w], t1[:, :w], AF.Exp, [t1B, ssB], [ssB], scale=-0.5)
            return ss, ssB

        def norm_stage(l, g, At, sh_off):
            rss = [rstd_block(g, c0, w, bidx) for bidx, (c0, w, kind) in enumerate(group_blocks(g))]
            for bidx, (c0, w, kind) in enumerate(group_blocks(g)):
                rs, rsB = rss[bidx]
                for k in range(8):
                    tmp, tmpB = FP.get()
                    if kind == "p":
                        stt(tmp[:, :w], xT[:, k, c0:c0 + w], At[:, k, 0:1], rs[:, :w], MUL, MUL,
                            [B("x", k, bidx), rsB, AB[l]], [tmpB])
                        act(hT[:, k, c0:c0 + w], tmp[:, :w], AF.Identity, [tmpB, modB[l]], [B("h", k, bidx)],
                            bias=modT[l][:, sh_off + k, 0:1])
                    else:
                        tt(tmp[:, :w], xT[:, k, c0:c0 + w], rs[:, :w], MUL, [B("x", k, bidx), rsB], [tmpB])
                        t3 = v3(tmp[:, :w], 8)
                        tt(t3, t3, bc3(At[:, k, 1:17], 8), MUL, [tmpB, AB[l]], [tmpB])
                        tt(v3(hT[:, k, c0:c0 + w], 8), t3, bc3(modT[l][:, sh_off + k, 1:17], 8), ADD,
                           [tmpB, modB[l]], [B("h", k, bidx)])

        def hbufs(bidx):
            return [B("h", k, bidx) for k in range(8)]

        def fm_proj(ps, w, wv, wb, col0, c0, bidx, kc=8, src=None, srcbufs=None):
            src = hT if src is None else src
            for k in range(kc):
                rb = [wb] + ([B("h", k, bidx)] if srcbufs is None else srcbufs(k))
                mm(ps[0][:, :w], wv[:, k, col0:col0 + 128], src[:, k, c0:c0 + w], k == 0, k == kc - 1, rb, [ps[1]])

        def gla_stage(l, g, extra=()):
            t0, nt, smp = GROUPS[g]
            blocks = group_blocks(g)
            n = group_n(g)
            ntile = nt + (1 if smp else 0)
            for bidx, (c0, w, kind) in enumerate(blocks):
                ps = PS.get()
                for k in range(8):
                    mm(ps[0][0:16, :w], wlri[l][:, k, :], hT[:, k, c0:c0 + w], k == 0, k == 7, [wsmB, B("h", k, bidx)], [ps[1]])
                cp(lrT[0:16, c0:c0 + w], ps[0][0:16, :w], [ps[1]], [lrTB], eng="act")
            for s in range(2):
                wv, wb = load_k8(dwin, l, C_OG + s * 512, 512)
                for cc in range(4):
                    for bidx, (c0, w, kind) in enumerate(blocks):
                        ps = PS.get()
                        fm_proj(ps, w, wv, wb, cc * 128, c0, bidx)
                        act(sogT[:, s * 4 + cc, c0:c0 + w], ps[0][:, :w], AF.Silu, [ps[1]], [B("sog", bidx)])
            for s in range(2):
                wv, wb = load_k8(dwin, l, C_V + s * 512, 512)
                for t in range(ntile):
                    bidx = min(t // 4, len(blocks) - 1) if not (smp and t == nt) else len(blocks) - 1
                    ps = PS.get()
                    for k in range(8):
                        mm(ps[0][:], hT[:, k, t * 128:(t + 1) * 128], wv[:, k, :], k == 0, k == 7, [wb, B("h", k, bidx)], [ps[1]])
                    cp(vTM[:, t, s * 512:(s + 1) * 512], ps[0][:], [ps[1]], [B("v", t)], eng="act")
            for f_ in extra:
                f_()
            wq, wqB = load_k8(dwin, l, C_Q, 512)
            wk, wkB = load_k8(dwin, l, C_K, 512)
            if smp:
                ms(qdm[:], 0.0, [qdmB])
                ms(smpT[:, 0:1], 0.0, [smpB])
            for h in range(4):
                if t0 == 0:
                    ms(S32[l][:, h, :], 0.0, [S32B[l][h]])
                cp(Sbf[:, h, :], S32[l][:, h, :], [S32B[l][h]], [SbfB[h]], eng="act")
            for bidx, (c0, w, kind) in enumerate(blocks):
                ntb = w // 128
                tl = 128 if kind == "p" else 8
                nseg = w // tl
                R = Rp if kind == "p" else Rs
                hst = {}

                def H1(h):
                    L = PS.get()
                    mm(L[0][:, :w], wlr[l][0:16, h * 128:(h + 1) * 128], lrT[0:16, c0:c0 + w], True, True, [wsmB, lrTB], [L[1]])
                    e1, e1B = FP.get()
                    act(e1[:, :w], L[0][:, :w], AF.Exp, [L[1], derB[l]], [e1B], scale=-1.0, bias=der[l][:, 24 + h:25 + h])
                    act(e1[:, :w], e1[:, :w], AF.Ln, [e1B], [e1B], bias=1.0)
                    cum, cumB = FP.get()
                    scan(cum[:, :w], R[:, :w], e1[:, :w], 0.0, [e1B, constB], [cumB])
                    hst[h] = (cum, cumB)

                def H2(h):
                    cum, cumB = hst[h]
                    E1, E1B = FP.get()
                    E2, E2B = FP.get()
                    act(E1[:, :w], cum[:, :w], AF.Exp, [cumB], [E1B], scale=-1.0 / 16)
                    act(E2[:, :w], cum[:, :w], AF.Exp, [cumB], [E2B], scale=1.0 / 16)
                    E3, E3B = FP.get()
                    tt(v3(E3[:, :w], tl), v3(E2[:, :w], tl), v3(E1[:, :w], tl)[:, :, tl - 1:tl].to_broadcast([128, nseg, tl]), MUL,
                       [E1B, E2B], [E3B])
                    cp(Dt[:, h, 0:nseg], v3(E1[:, :w], tl)[:, :, tl - 1], [E1B], [DtB[h]])
                    Q = PS.get()
                    fm_proj(Q, w, wq, wqB, h * 128, c0, bidx)
                    stt(QD[h][0][:, :w], Q[0][:, :w], float(DK) ** -0.5, E1[:, :w], MUL, MUL, [Q[1], E1B], [QD[h][1]])
                    K = PS.get()
                    fm_proj(K, w, wk, wkB, h * 128, c0, bidx)
                    tt(KD[h][0][:, :w], K[0][:, :w], E2[:, :w], MUL, [K[1], E2B], [KD[h][1]])
                    tt(KE[h][0][:, :w], K[0][:, :w], E3[:, :w], MUL, [K[1], E3B], [KE[h][1]])

                H1(0)
                for h in range(4):
                    if h + 1 < 4:
                        H1(h + 1)
                    H2(h)
                qd, kd, ke = QD, KD, KE
                mask = maskP if kind == "p" else maskS
                tst = {}

                def TA(j):
                    cs = slice(j * 128, (j + 1) * 128)
                    ATb = PS.get()
                    KTb = PS.get()
                    kes_t, kes_B = BP.get()
                    atm_t, atm_B = BP.get()
                    for h in range(4):
                        KT = KTb[0][:, 128 * h:128 * h + 64].bitcast(BF16)
                        tr(KT, ke[h][0][:, cs], [ke[h][1], constB], [KTb[1][h]])
                        mm(ATb[0][:, 128 * h:128 * h + 128], kd[h][0][:, cs], qd[h][0][:, cs], True, True, [kd[h][1], qd[h][1]], [ATb[1][h]])
                    for h in range(4):
                        KT = KTb[0][:, 128 * h:128 * h + 64].bitcast(BF16)
                        cp(kes_t[:, 128 * h:128 * h + 128], KT, [KTb[1][h]], [kes_B], eng="act")
                        tt(atm_t[:, 128 * h:128 * h + 128], ATb[0][:, 128 * h:128 * h + 128], mask[:], MUL, [ATb[1][h], constB], [atm_B])
                    tst[j] = (kes_t, kes_B, atm_t, atm_B)

                def TB(j):
                    t = c0 // 128 + j
                    cs = slice(j * 128, (j + 1) * 128)
                    kes_t, kes_B, atm_t, atm_B = tst[j]
                    KVb = [PS.get(), PS.get()]
                    for h in range(4):
                        ob = OB[h // 2]
                        osl = Ops[:, h * 256:(h + 1) * 256]
                        vsl = vTM[:, t, h * 256:(h + 1) * 256]
                        mm(osl, atm_t[:, 128 * h:128 * h + 128], vsl, True, False, [atm_B, B("v", t)], [ob])
                        mm(osl, qd[h][0][:, cs], Sbf[:, h, :], False, True, [qd[h][1], SbfB[h]], [ob])
                        kvb = KVb[h % 2]
                        hq = 2 * (h // 2)
                        KV = kvb[0][:, 128 * hq:128 * hq + 256]
                        kvB = [kvb[1][hq], kvb[1][hq + 1]]
                        mm(KV, kes_t[:, 128 * h:128 * h + 128], vsl, True, True, [kes_B, B("v", t)], kvB)
                        stt(S32[l][:, h, :], S32[l][:, h, :], Dt[:, h, j:j + 1], KV, MUL, ADD, [kvB, DtB[h], S32B[l][h]], [S32B[l][h]])
                        cp(Sbf[:, h, :], S32[l][:, h, :], [S32B[l][h]], [SbfB[h]], eng="act")

                def TBs(j):
                    t = c0 // 128 + j
                    cs = slice(j * 128, (j + 1) * 128)
                    kes_t, kes_B, atm_t, atm_B = tst[j]
                    for h in range(4):
                        ob = OB[h // 2]
                        osl = Ops[:, h * 256:(h + 1) * 256]
                        vsl = vTM[:, t, h * 256:(h + 1) * 256]
                        qdiag = bass.AP(U, 6144, [[28 * NMAX, 128], [136, 16], [1, 8]])
                        cp(qdiag, v3(qd[h][0][:, cs], 8), [qd[h][1]], [qdmB])
                        mm(osl, atm_t[:, 128 * h:128 * h + 128], vsl, True, False, [atm_B, B("v", t)], [ob])
                        for q4 in range(16 // NSQ):
                            sf, sfB, sfX = S0f.get()
                            s0b, s0bB, s0bX = S0b.get()
                            src = dsg[l, NSQ * q4:NSQ * q4 + NSQ, h].rearrange("s k v -> k s v")
                            P.dma("sp", sf, src, sfB, reads=sfX, writes=[sfB])
                            P.dma("pool", s0b, src, s0bB, reads=s0bX, writes=[s0bB])
                            for jj in range(NSQ):
                                sq_ = NSQ * q4 + jj
                                mm(osl, qdm[:, sq_, :], s0b[:, jj, :], False, sq_ == 15, [qdmB, s0bB] + s0bX, [ob])
                            kms, kvs = [], []
                            for jj in range(NSQ):
                                sq_ = NSQ * q4 + jj
                                km_t, km_B = KM.get()
                                ts(km_t[:], kes_t[:, 128 * h:128 * h + 128], M16[:, sq_:sq_ + 1], None, MUL, None, [kes_B, constB], [km_B])
                                kms.append((km_t, km_B))
                            for jj in range(NSQ):
                                kvb = PS.get()
                                mm(kvb[0][:, 0:256], kms[jj][0][:], vsl, True, True, [kms[jj][1], B("v", t)], [kvb[1]])
                                kvs.append(kvb)
                            for jj in range(NSQ):
                                sq_ = NSQ * q4 + jj
                                stt(sf[:, jj, :], sf[:, jj, :], Dt[:, h, sq_:sq_ + 1], kvs[jj][0][:, 0:256], MUL, ADD,
                                    [kvs[jj][1], DtB[h], sfB] + sfX, [sfB])
                            P.dma("act", ogl[l, 1 + NSQ * q4:1 + NSQ * q4 + NSQ, h].rearrange("s k v -> k s v"), sf, sfB,
                                  reads=[sfB] + sfX, is_out=True)

                def TN1(j):
                    po = 12 * (j % 2)
                    sB = ssqB[j % 2]
                    oa, oaB = FP.get()
                    ob_, obB = FP.get()
                    cp(oa[:], Ops[:, 0:512], [OB[0]], [oaB], eng="act")
                    cp(ob_[:], Ops[:, 512:1024], [OB[1]], [obB], eng="act")
                    for h in range(4):
                        src_t, src_B = (oa, oaB) if h < 2 else (ob_, obB)
                        act(junk[:], src_t[:, (h % 2) * 256:(h % 2) * 256 + 256], AF.Square, [src_B, junkB], [junkB, sB],
                            accum=ssq[:, po + h:po + h + 1])
                    act(ssq[:, po + 4:po + 8], ssq[:, po:po + 4], AF.Ln, [sB, constB], [sB], scale=1.0 / DV, bias=epsc[:, 0:1])
                    act(ssq[:, po + 8:po + 12], ssq[:, po + 4:po + 8], AF.Exp, [sB], [sB], scale=-0.5)
                    tst[("o", j)] = (oa, oaB, ob_, obB)

                def TN2a(j):
                    po = 12 * (j % 2)
                    sB = ssqB[j % 2]
                    oa, oaB, ob_, obB = tst[("o", j)]
                    on_t, on_B = FP.get()
                    onb = on_t[:].bitcast(BF16)
                    for h in range(4):
                        src_t, src_B = (oa, oaB) if h < 2 else (ob_, obB)
                        stt(onb[:, h * 256:(h + 1) * 256], src_t[:, (h % 2) * 256:(h % 2) * 256 + 256], ssq[:, po + 8 + h:po + 9 + h], gn[l][:], MUL, MUL,
                            [src_B, sB, gnB[l]], [on_B])
                    tst[("on", j)] = (onb, on_B)

                def TN2b(j):
                    onb, on_B = tst[("on", j)]
                    yps = PS.get()
                    ypv = yps[0][:].bitcast(BF16).rearrange("p (c t) -> p c t", t=128)
                    for c in range(8):
                        tr(ypv[:, c, :], onb[:, c * 128:(c + 1) * 128], [on_B, constB], [yps[1]])
                    tt(yBT[:, 0:8, c0 + j * 128:c0 + (j + 1) * 128], ypv, sogT[:, 0:8, c0 + j * 128:c0 + (j + 1) * 128], MUL,
                       [yps[1], B("sog", bidx)], [B("yB", bidx)])

                TA(0)
                for j in range(ntb):
                    if kind == "p":
                        TB(j)
                    else:
                        TBs(j)
                    if j + 1 < ntb:
                        TA(j + 1)
                    TN1(j)
                    if j >= 1:
                        TN2a(j - 1)
                    if j >= 2:
                        TN2b(j - 2)
                TN2a(ntb - 1)
                for jj in range(max(0, ntb - 2), ntb):
                    TN2b(jj)
            if t0 + nt == 16:
                for h in range(4):
                    P.dma("sp", ogl[l, 0, h], S32[l][:, h, :], S32B[l][h], reads=[S32B[l][h]], is_out=True)

        def lru_stage(l, g):
            t0, nt, smp = GROUPS[g]
            blocks = group_blocks(g)
            FPT = FP.items
            if smp:
                P.dma("sp", lhs_t, dlh[l], lhsB, writes=[lhsB])
                P.dma("sp", lcs_t, dlc[l], lcsB, writes=[lcsB])
            if t0 == 0:
                for c in range(12):
                    ms(LH[l][:, c, :], 0.0, [LHB[l][c]])
                    ms(HS[l][:, c:c + 1], 0.0, [HSB[l][c]])
            cw = vec[l][:, V_LCW:V_LCW + 48]
            wts = {}

            def get_w(g3):
                if g3 not in wts:
                    wxl_t, wxlB = load_slab([(tiled_dst(0, 3, 8), tiled_src(dwxg, l, 3 * g3, 3)),
                                             (lambda t: t[:, 3072:3968].rearrange("p (k n) -> p k n", n=128), dwa[l][:, 7 * g3:7 * g3 + 7, :])])
                    wg_t, wgB = load_slab([(tiled_dst(0, 3, 8), tiled_src(dwxg, l, 12 + 3 * g3, 3)),
                                           (lambda t: t[:, 3072:3968].rearrange("p (k n) -> p k n", n=128), dwx[l][:, 7 * g3:7 * g3 + 7, :])])
                    wts[g3] = (wxl_t, wxlB, wg_t, wgB)
                return wts[g3]

            tbs = [(g3, bidx) for g3 in range(4) for bidx in range(len(blocks))]

            def p1a(ti):
                g3, bidx = tbs[ti]
                c0, w, kind = blocks[bidx]
                wxl_t, wxlB, _, _ = get_w(g3)
                st = ti % 2
                for ci in range(3):
                    c = 3 * g3 + ci
                    ps = PS8.get()
                    fm_proj(ps, w, tiled_view(wxl_t, 0, ci, 8), wxlB, 0, c0, bidx)
                    xp, xpB = XPp.get()
                    xc, xcB = XCs[st][ci]
                    cb = vec[l][:, V_LCB + c:V_LCB + c + 1]
                    if kind == "p":
                        cp(xp[:, 0:3], LH[l][:, c, :], [LHB[l][c]], [xpB], eng="act")
                        cp(xp[:, 3:3 + w], ps[0][:, :w], [ps[1]], [xpB], eng="act")
                        cp(LH[l][:, c, :], xp[:, w:w + 3], [xpB], [LHB[l][c]], eng="act")
                        ts(xc[:, :w], xp[:, 3:3 + w], cw[:, 3 * 12 + c:3 * 12 + c + 1], cb, MUL, ADD, [xpB, vecB[l]], xcB)
                        for jt in range(3):
                            stt(xc[:, :w], xp[:, jt:jt + w], cw[:, jt * 12 + c:jt * 12 + c + 1], xc[:, :w], MUL, ADD,
                                [xpB, vecB[l]] + xcB, xcB)
                    else:
                        xp3 = xp[:, 0:176].rearrange("p (j t) -> p j t", t=11)
                        cp(xp3[:, :, 0:3], lcs_t[:, c, :, :], [lcsB], [xpB], eng="act")
                        cp(xp3[:, :, 3:11], v3(ps[0][:, :w], 8), [ps[1]], [xpB], eng="act")
                        cp(lco_t[:, c, :, :], xp3[:, :, 8:11], [xpB], [lcoB], eng="act")
                        xc3 = v3(xc[:, :w], 8)
                        ts(xc3, xp3[:, :, 3:11], cw[:, 3 * 12 + c:3 * 12 + c + 1], cb, MUL, ADD, [xpB, vecB[l]], xcB)
                        for jt in range(3):
                            stt(xc3, xp3[:, :, jt:jt + 8], cw[:, jt * 12 + c:jt * 12 + c + 1], xc3, MUL, ADD,
                                [xpB, vecB[l]] + xcB, xcB)

            def p1b(ti):
                g3, bidx = tbs[ti]
                c0, w, kind = blocks[bidx]
                st = ti % 2
                for ci in range(3):
                    cp(XCBs[st][ci][0][:, :w], XCs[st][ci][0][:, :w], XCs[st][ci][1], XCBs[st][ci][1], eng="act")

            def p2(ti, mid):
                g3, bidx = tbs[ti]
                c0, w, kind = blocks[bidx]
                wxl_t, wxlB, wg_t, wgB = get_w(g3)
                wa = wxl_t[:, 3072:3968].rearrange("p (k n) -> p k n", n=128)
                wx = wg_t[:, 3072:3968].rearrange("p (k n) -> p k n", n=128)
                st = ti % 2
                XC, XCB = XCs[st], XCBs[st]
                Rl, Il, Gl = [], [], []
                for ci in range(3):
                    c = 3 * g3 + ci
                    prs = [(pi - 7 * g3, k) for pi, (k, m) in enumerate(LRU_PAIRS) if m == c]
                    Rps = PS8.get()
                    for ii, (pi, k) in enumerate(prs):
                        xb_t, xb_B = XCB[k - 3 * g3]
                        mm(Rps[0][:, :w], wa[:, pi, :], xb_t[:, :w], ii == 0, ii == len(prs) - 1, [wxlB] + xb_B, [Rps[1]])
                    Ips = PS8.get()
                    for ii, (pi, k) in enumerate(prs):
                        xb_t, xb_B = XCB[k - 3 * g3]
                        mm(Ips[0][:, :w], wx[:, pi, :], xb_t[:, :w], ii == 0, ii == len(prs) - 1, [wgB] + xb_B, [Ips[1]])
                    Rl.append(Rps); Il.append(Ips)
                for ci in range(3):
                    c = 3 * g3 + ci
                    T, TB = FPT[3 * ci]
                    act(T[:, :w], Rl[ci][0][:, :w], AF.Tanh, [Rl[ci][1], derB[l]], [TB], scale=0.5, bias=der[l][:, 28 + c:29 + c])
                for ci in range(3):
                    c = 3 * g3 + ci
                    I_, IB = FPT[3 * ci + 2]
                    act(I_[:, :w], Il[ci][0][:, :w], AF.Tanh, [Il[ci][1], derB[l]], [IB], scale=0.5, bias=der[l][:, 40 + c:41 + c])
                for ci in range(3):
                    c = 3 * g3 + ci
                    T, TB = FPT[3 * ci]
                    M_, MB = FPT[3 * ci + 1]
                    act(M_[:, :w], T[:, :w], AF.Exp, [TB, derB[l]], [MB], scale=der[l][:, 12 + c:13 + c], bias=der[l][:, 12 + c:13 + c])
                    act(T[:, :w], T[:, :w], AF.Exp, [TB, derB[l]], [TB], scale=der[l][:, c:c + 1], bias=der[l][:, c:c + 1])
                for ci in range(3):
                    M_, MB = FPT[3 * ci + 1]
                    act(M_[:, :w], M_[:, :w], AF.Sqrt, [MB], [MB], scale=-1.0, bias=1.0)
                mid()
                for ci in range(3):
                    Gps = PS8.get()
                    fm_proj(Gps, w, tiled_view(wg_t, 0, ci, 8), wgB, 0, c0, bidx)
                    Gl.append(Gps)
                for ci in range(3):
                    M_, MB = FPT[3 * ci + 1]
                    I_, IB = FPT[3 * ci + 2]
                    xc, xcB = XC[ci]
                    if kind == "p" and t0 == 0 and c0 == 0:
                        ms(M_[:, 0:1], 1.0, [MB], r=[MB])
                    stt(M_[:, :w], I_[:, :w], 1.0, M_[:, :w], ADD, MUL, [IB, MB], [MB])
                    stt(M_[:, :w], M_[:, :w], 0.5, xc[:, :w], MUL, MUL, [MB] + xcB, [MB])
                for ci in range(3):
                    I_, IB = FPT[3 * ci + 2]
                    act(I_[:, :w], Gl[ci][0][:, :w], AF.Gelu_apprx_tanh, [Gl[ci][1], IB], [IB])
                for ci in range(3):
                    c = 3 * g3 + ci
                    T, TB = FPT[3 * ci]
                    M_, MB = FPT[3 * ci + 1]
                    I_, IB = FPT[3 * ci + 2]
                    h_t, hB_ = XC[ci]
                    if kind == "p":
                        scan(h_t[:, :w], T[:, :w], M_[:, :w], HS[l][:, c:c + 1], [TB, MB, HSB[l][c]], hB_)
                        cp(HS[l][:, c:c + 1], h_t[:, w - 1:w], hB_, [HSB[l][c]])
                    else:
                        a3 = v3(T[:, :w], 8)
                        b3 = v3(M_[:, :w], 8)
                        tmp, tmpB = FPT[9]
                        tt(tmp[:, 0:16], a3[:, :, 0], lhs_t[:, c, :], MUL, [TB, lhsB], [tmpB])
                        tt(b3[:, :, 0], b3[:, :, 0], tmp[:, 0:16], ADD, [MB, tmpB], [MB])
                        ms(a3[:, :, 0:1], 0.0, [TB], r=[TB, tmpB])
                        scan(h_t[:, :w], T[:, :w], M_[:, :w], 0.0, [TB, MB], hB_)
                        cp(lho_t[:, c, :], v3(h_t[:, :w], 8)[:, :, 7], hB_, [lhoB])
                    tt(yAT[:, c, c0:c0 + w], h_t[:, :w], I_[:, :w], MUL, hB_ + [IB], [B("yA", c, bidx)])

            p1a(0)
            p1b(0)
            for ti in range(len(tbs)):
                nxt = ti + 1 < len(tbs)
                if nxt:
                    p1a(ti + 1)
                p2(ti, (lambda ti=ti: p1b(ti + 1)) if nxt else (lambda: None))
            if smp:
                P.dma("sp", olhs[l], lho_t, lhoB, reads=[lhoB], is_out=True)
                P.dma("sp", olcs[l], lco_t, lcoB, reads=[lcoB], is_out=True)
            if t0 + nt == 16:
                P.dma("sp", olhp[l], HS[l][:], HSB[l][0], reads=HSB[l], is_out=True)
                P.dma("sp", olcp[l], LH[l][:], LHB[l][0], reads=LHB[l], is_out=True)

        def merge_stage(l, g):
            blocks = group_blocks(g)
            for m in range(8):
                sx_t, sxB = load_slab([(tiled_dst(0, 1, 12), tiled_src(dwba, l, m, 1)),
                                       (tiled_dst(1536, 1, 8), tiled_src(dwbb, l, m, 1))])
                sa = tiled_view(sx_t, 0, 0, 12)
                sbv = tiled_view(sx_t, 1536, 0, 8)
                sg_t, sgB = SCg.get()
                P.dma("pool", tiled_dst(0, 1, 8)(sg_t), tiled_src(dwmg, l, m, 1), sgB, writes=[sgB], after_barrier=True)
                P.dma("pool", tiled_dst(1024, 1, 8)(sg_t), tiled_src(dwmg, l, 8 + m, 1), sgB, writes=[sgB], join=True, after_barrier=True)
                sga = tiled_view(sg_t, 0, 0, 8)
                sgb = tiled_view(sg_t, 1024, 0, 8)
                for bidx, (c0, w, kind) in enumerate(blocks):
                    PA = PS8.get()
                    fm_proj(PA, w, sa, sxB, 0, c0, bidx, kc=12, src=yAT, srcbufs=lambda k: [B("yA", k, bidx)])
                    PB = PS8.get()
                    fm_proj(PB, w, sbv, sxB, 0, c0, bidx, kc=8, src=yBT, srcbufs=lambda k: [B("yB", bidx)])
                    GA = PS8.get()
                    fm_proj(GA, w, sga, sgB, 0, c0, bidx)
                    GB = PS8.get()
                    fm_proj(GB, w, sgb, sgB, 0, c0, bidx)
                    ga, gaB = FP.get()
                    act(ga[:, :w], GA[0][:, :w], AF.Sigmoid, [GA[1]], [gaB])
                    gb, gbB = FP.get()
                    act(gb[:, :w], GB[0][:, :w], AF.Sigmoid, [GB[1]], [gbB])
                    tt(ga[:, :w], ga[:, :w], PA[0][:, :w], MUL, [gaB, PA[1]], [gaB])
                    tt(gb[:, :w], gb[:, :w], PB[0][:, :w], MUL, [gbB, PB[1]], [gbB])
                    tt(mrgT[:, m, c0:c0 + w], ga[:, :w], gb[:, :w], ADD, [gaB, gbB], [B("mrg", m, bidx)])

        def resid_update(l, ps, mo, c0, w, kind, bidx, gate_off):
            if kind == "p":
                stt(xT[:, mo, c0:c0 + w], ps[0][:, :w], modT[l][:, gate_off + mo, 0:1], xT[:, mo, c0:c0 + w], MUL, ADD,
                    [ps[1], modB[l], B("x", mo, bidx)], [B("x", mo, bidx)])
            else:
                tmp, tmpB = FP.get()
                tt(v3(tmp[:, :w], 8), v3(ps[0][:, :w], 8), bc3(modT[l][:, gate_off + mo, 1:17], 8), MUL, [ps[1], modB[l]], [tmpB])
                tt(xT[:, mo, c0:c0 + w], xT[:, mo, c0:c0 + w], tmp[:, :w], ADD, [tmpB, B("x", mo, bidx)], [B("x", mo, bidx)])

        def out_stage(l, g):
            blocks = group_blocks(g)
            wvs = [load_k8(dwo, l, s * 512, 512) for s in range(2)]
            for bidx, (c0, w, kind) in enumerate(blocks):
                for mo in range(8):
                    wv, wb = wvs[mo // 4]
                    ps = PS8.get()
                    fm_proj(ps, w, wv, wb, (mo % 4) * 128, c0, bidx, src=mrgT, srcbufs=lambda k: [B("mrg", k, bidx)])
                    resid_update(l, ps, mo, c0, w, kind, bidx, 16)

        def ffn_stage(l, g, extra=()):
            extra = list(extra)
            t0, nt, smp = GROUPS[g]
            blocks = group_blocks(g)
            if smp:
                P.dma("sp", fcs_t, dfc[l], fcsB, writes=[fcsB])
            if t0 == 0:
                for f in range(22):
                    ms(FH[l][:, f, :], 0.0, [FHB[l][f]])
            cw = vec[l][:, V_FCW:V_FCW + 66]
            pend = []

            def stage_b(it):
                cv, cvB, UV, f, c0, w, bidx = it
                act(cv[:, :w], cv[:, :w], AF.Silu, [cvB], [cvB])
                tt(actT[:, f, c0:c0 + w], cv[:, :w], UV[0][:, :w], MUL, [cvB, UV[1]], [B("act", f, bidx)])

            for j in range(11):
                su_t, suB = load_slab([(tiled_dst(0, 2, 8), tiled_src(dwup, l, 2 * j, 2)),
                                       (tiled_dst(2048, 2, 8), tiled_src(dwup, l, 22 + 2 * j, 2))])
                for ci in range(2):
                    f = 2 * j + ci
                    cb = vec[l][:, V_FCB + f:V_FCB + f + 1]
                    for bidx, (c0, w, kind) in enumerate(blocks):
                        UG = PS8.get()
                        fm_proj(UG, w, tiled_view(su_t, 0, ci, 8), suB, 0, c0, bidx)
                        UV = PS8.get()
                        fm_proj(UV, w, tiled_view(su_t, 2048, ci, 8), suB, 0, c0, bidx)
                        xp, xpB = XPp.get()
                        cv, cvB = FP.get()
                        if kind == "p":
                            cp(xp[:, 0:2], FH[l][:, f, :], [FHB[l][f]], [xpB], eng="act")
                            cp(xp[:, 2:2 + w], UG[0][:, :w], [UG[1]], [xpB], eng="act")
                            cp(FH[l][:, f, :], xp[:, w:w + 2], [xpB], [FHB[l][f]], eng="act")
                            ts(cv[:, :w], xp[:, 2:2 + w], cw[:, 2 * 22 + f:2 * 22 + f + 1], cb, MUL, ADD, [xpB, vecB[l]], [cvB])
                            for jt in range(2):
                                stt(cv[:, :w], xp[:, jt:jt + w], cw[:, jt * 22 + f:jt * 22 + f + 1], cv[:, :w], MUL, ADD,
                                    [xpB, vecB[l], cvB], [cvB])
                        else:
                            xp3 = xp[:, 0:160].rearrange("p (j t) -> p j t", t=10)
                            cp(xp3[:, :, 0:2], fcs_t[:, f, :, :], [fcsB], [xpB], eng="act")
                            cp(xp3[:, :, 2:10], v3(UG[0][:, :w], 8), [UG[1]], [xpB], eng="act")
                            cp(fco_t[:, f, :, :], xp3[:, :, 8:10], [xpB], [fcoB], eng="act")
                            cv3 = v3(cv[:, :w], 8)
                            ts(cv3, xp3[:, :, 2:10], cw[:, 2 * 22 + f:2 * 22 + f + 1], cb, MUL, ADD, [xpB, vecB[l]], [cvB])
                            for jt in range(2):
                                stt(cv3, xp3[:, :, jt:jt + 8], cw[:, jt * 22 + f:jt * 22 + f + 1], cv3, MUL, ADD,
                                    [xpB, vecB[l], cvB], [cvB])
                        pend.append((cv, cvB, UV, f, c0, w, bidx))
                        if len(pend) > 1:
                            stage_b(pend.pop(0))
                if extra:
                    extra.pop(0)()
            while pend:
                stage_b(pend.pop(0))
            if smp:
                P.dma("sp", ofcs[l], fco_t, fcoB, reads=[fcoB], is_out=True)
            if t0 + nt == 16:
                P.dma("sp", ofcp[l], FH[l][:], FHB[l][0], reads=FHB[l], is_out=True)
            for mo in range(8):
                wd_t, wdB = load_slab([(lambda t: t[:, 0:1408], dwdn[l, mo].rearrange("p k n -> p (k n)")[:, 0:1408]),
                                       (lambda t: t[:, 1408:2816], dwdn[l, mo].rearrange("p k n -> p (k n)")[:, 1408:2816])])
                wd = slab_k(wd_t, 22, 128)
                for bidx, (c0, w, kind) in enumerate(blocks):
                    ps = PS8.get()
                    fm_proj(ps, w, wd, wdB, 0, c0, bidx, kc=22, src=actT, srcbufs=lambda k: [B("act", k, bidx)])
                    resid_update(l, ps, mo, c0, w, kind, bidx, 40)
                if extra:
                    extra.pop(0)()
            while extra:
                extra.pop(0)()

        def final_stage(g):
            rss = [rstd_block(g, c0, w, bidx) for bidx, (c0, w, kind) in enumerate(group_blocks(g))]
            for bidx, (c0, w, kind) in enumerate(group_blocks(g)):
                rs, rsB = rss[bidx]
                for k in range(8):
                    yo, yoB = FP.get()
                    stt(yo[:, :w], xT[:, k, c0:c0 + w], vec[1][:, V_GFIN + k:V_GFIN + k + 1], rs[:, :w], MUL, MUL,
                        [B("x", k, bidx), rsB, vecB[1]], [yoB])
                    P.dma("sp", oy[g][:, k, c0:c0 + w], yo[:, :w], yoB, reads=[yoB], is_out=True)

        epsc = sb("epsc", (128, 1))
        ms(epsc[:], EPS, [constB], eng="pool")
        for g in range(NG):
            n = group_n(g)
            xall = Buf("xload")
            for k in range(8):
                bl = [B("x", k, bidx) for bidx in range(len(group_blocks(g)))]
                P.dma("sp", xT[:, k, 0:n], dx[g][:, k, :], B("xdma", k), writes=bl)
            for l in range(2):
                late0 = ()
                if g == 0 and l == 0:
                    P.mark("mod g%d l%d" % (g, l))
                    for s_ in range(4):
                        mod_slab(0, s_)
                    mod_fin(0, which=(0,))
                    late0 = [(lambda s_=s_: mod_slab(0, s_)) for s_ in range(4, 12)] + [lambda: mod_fin(0, which=(1,))]
                P.mark("norm1 g%d l%d" % (g, l))
                norm_stage(l, g, A1[l], 0)
                P.barrier()
                P.mark("gla g%d l%d" % (g, l))
                gla_stage(l, g, extra=late0)
                P.barrier()
                P.mark("lru g%d l%d" % (g, l))
                lru_stage(l, g)
                P.barrier()
                P.mark("merge g%d l%d" % (g, l))
                merge_stage(l, g)
                P.mark("out g%d l%d" % (g, l))
                out_stage(l, g)
                P.mark("norm2 g%d l%d" % (g, l))
                norm_stage(l, g, A2[l], 24)
                P.barrier()
                P.mark("ffn g%d l%d" % (g, l))
                ffn_stage(l, g, extra=mod_steps(1) if (g == 0 and l == 0) else ())
            P.mark("final g%d" % g)
            final_stage(g)
        P.mark("end")
        stats = P.emit(ctx)
    return nc, (stats, P.marks)


_CACHE = {}


def _fm(a, nchunks):
    a = np.asarray(a)
    lead = a.shape[:-1]
    a = a.reshape(lead + (nchunks, 128))
    nd = a.ndim
    perm = (nd - 1, nd - 2) + tuple(range(nd - 2))
    return np.ascontiguousarray(a.transpose(perm))


def kernel(x_prompt, x_sample, c_prompt, c_sample, state_lru_h, state_lru_conv, state_gla, state_ffn_conv,
           ada_w, ada_b, norm_mix_g, norm_ffn_g, w_in, lru_conv_w, lru_conv_b, lru_wa, lru_ba, lru_wx, lru_bx,
           lru_lambda, gla_w_lr, gla_b_lr, gla_norm_g, w_branch_a, w_branch_b, w_out, ffn_w_up, ffn_conv_w,
           ffn_conv_b, ffn_w_down, final_norm_g):
    f32 = np.float32
    if "nc" not in _CACHE:
        _CACHE["nc"] = build_nc()
    nc, stats = _CACHE["nc"]

    vec = np.zeros((2, 128, NV), f32)
    for l in range(2):
        vec[l, :, V_ADAB:V_ADAB + 48] = np.asarray(ada_b[l], f32).reshape(48, 128).T
        vec[l, :, V_GMIX:V_GMIX + 8] = np.asarray(norm_mix_g[l], f32).reshape(8, 128).T
        vec[l, :, V_GFFN:V_GFFN + 8] = np.asarray(norm_ffn_g[l], f32).reshape(8, 128).T
        vec[l, :, V_LCW:V_LCW + 48] = np.asarray(lru_conv_w[l], f32).reshape(4, 12, 128).transpose(2, 0, 1).reshape(128, 48)
        vec[l, :, V_LCB:V_LCB + 12] = np.asarray(lru_conv_b[l], f32).reshape(12, 128).T
        vec[l, :, V_LBA:V_LBA + 12] = np.asarray(lru_ba[l], f32).reshape(12, 128).T
        vec[l, :, V_LBX:V_LBX + 12] = np.asarray(lru_bx[l], f32).reshape(12, 128).T
        vec[l, :, V_LLAM:V_LLAM + 12] = np.asarray(lru_lambda[l], f32).reshape(12, 128).T
        vec[l, :, V_BLR:V_BLR + 4] = np.asarray(gla_b_lr[l], f32).reshape(4, 128).T
        vec[l, :, V_FCW:V_FCW + 66] = np.asarray(ffn_conv_w[l], f32).reshape(3, 22, 128).transpose(2, 0, 1).reshape(128, 66)
        vec[l, :, V_FCB:V_FCB + 22] = np.asarray(ffn_conv_b[l], f32).reshape(22, 128).T
        vec[l, :, V_GFIN:V_GFIN + 8] = np.asarray(final_norm_g, f32).reshape(8, 128).T
    gnb = np.ascontiguousarray(np.broadcast_to(np.asarray(gla_norm_g, f32)[:, None, :], (2, 128, 256)))

    def expand_gate(wg):
        wg = np.asarray(wg, f32)
        out = np.zeros((2, 128, 28, 128), f32)
        for l in range(2):
            full = np.zeros((LW, LW), f32)
            for nb in range(8):
                full[nb * 192:(nb + 1) * 192, nb * 192:(nb + 1) * 192] = wg[l, nb]
            for pi, (k, m) in enumerate(LRU_PAIRS):
                out[l, :, pi, :] = full[k * 128:(k + 1) * 128, m * 128:(m + 1) * 128]
        return out

    def retile(wm):
        wm = np.asarray(wm, f32)
        L, K, M = wm.shape
        return np.ascontiguousarray(wm.reshape(L, K // 128, 128, M // 128, 128).transpose(0, 3, 2, 1, 4))

    shared = {
        "vec": vec, "gn": gnb,
        "ada_w": np.ascontiguousarray(ada_w, f32), "w_in": np.ascontiguousarray(w_in, f32),
        "wa_x": expand_gate(lru_wa), "wx_x": expand_gate(lru_wx),
        "w_lr": np.ascontiguousarray(gla_w_lr, f32),
        "w_ba_t": retile(w_branch_a), "w_bb_t": retile(w_branch_b),
        "w_mg_t": retile(np.asarray(w_in, f32)[:, :, C_MGA:C_MGA + 2048]),
        "w_xg_t": retile(np.asarray(w_in, f32)[:, :, C_XL:C_XL + 3072]),
        "w_o": np.ascontiguousarray(w_out, f32), "w_up_t": retile(ffn_w_up),
        "w_dn_t": retile(ffn_w_down),
    }
    xp = np.asarray(x_prompt, f32)
    xs = np.asarray(x_sample, f32)
    in_maps = []
    for c in range(NCORES):
        m = dict(shared)
        sl = slice(16 * c, 16 * c + 16)
        xpT = _fm(xp[c], 8)
        xsT = _fm(xs[sl].reshape(128, D), 8)
        for g, (t0, nt, smp) in enumerate(GROUPS):
            parts = [xpT[:, :, t0 * 128:(t0 + nt) * 128]]
            if smp:
                parts.append(xsT)
            m["x%d" % g] = np.ascontiguousarray(np.concatenate(parts, axis=2))
        call = np.concatenate([np.asarray(c_prompt, f32)[c:c + 1], np.asarray(c_sample, f32)[sl]], axis=0)
        m["cT"] = _fm(call, 8)
        m["lh_s"] = np.stack([_fm(np.asarray(state_lru_h, f32)[l, sl], 12) for l in range(2)])
        m["lc_s"] = np.stack([_fm(np.asarray(state_lru_conv, f32)[l, sl], 12) for l in range(2)])
        m["fc_s"] = np.stack([_fm(np.asarray(state_ffn_conv, f32)[l, sl], 22) for l in range(2)])
        m["sg_s"] = np.ascontiguousarray(np.asarray(state_gla, f32)[:, sl])
        in_maps.append(m)

    res = run_bass_kernel_spmd(nc, in_maps, core_ids=list(range(NCORES)))
    R = res.results

    def unfm(a):
        nd = a.ndim
        perm = tuple(range(2, nd)) + (1, 0)
        b = a.transpose(perm)
        return b.reshape(b.shape[:-2] + (b.shape[-2] * 128,))

    y_prompt = np.zeros((8, 2048, D), f32)
    y_sample = np.zeros((128, 8, D), f32)
    lru_h_p = np.zeros((2, 8, LW), f32); lru_h_s = np.zeros((2, 128, LW), f32)
    lru_c_p = np.zeros((2, 8, 3, LW), f32); lru_c_s = np.zeros((2, 128, 3, LW), f32)
    gla_p = np.zeros((2, 8, NH, DK, DV), f32); gla_s = np.zeros((2, 128, NH, DK, DV), f32)
    ffn_p = np.zeros((2, 8, 2, DFF), f32); ffn_s = np.zeros((2, 128, 2, DFF), f32)
    for c in range(NCORES):
        r = R[c]
        sl = slice(16 * c, 16 * c + 16)
        for g, (t0, nt, smp) in enumerate(GROUPS):
            yg = unfm(r["y%d" % g])
            y_prompt[c, t0 * 128:(t0 + nt) * 128] = yg[0:nt * 128]
            if smp:
                y_sample[sl] = yg[nt * 128:nt * 128 + 128].reshape(16, 8, D)
        for l in range(2):
            lru_h_p[l, c] = unfm(r["o_lh_p"][l])
            lru_h_s[l, sl] = unfm(r["o_lh_s"][l])
            lru_c_p[l, c] = unfm(r["o_lc_p"][l])
            lru_c_s[l, sl] = unfm(r["o_lc_s"][l])
            ffn_p[l, c] = unfm(r["o_fc_p"][l])
            ffn_s[l, sl] = unfm(r["o_fc_s"][l])
            gla_p[l, c] = r["o_gl"][l, 0]; gla_s[l, sl] = r["o_gl"][l, 1:17]
    return (y_prompt, y_sample, lru_h_p, lru_h_s, lru_c_p, lru_c_s, gla_p, gla_s, ffn_p, ffn_s)
```
